# Optimizing a Trainium2 kernel written in Bass

```python
import jax, jax.numpy as jnp
from jax import lax
import numpy as np

D_MODEL = 2048
BATCH = 4
SEQ = 4096
DEPTH = 4

N_MIXERS = 2
N_A = (DEPTH + 1) // 2
N_B = DEPTH // 2
CONF_KERNEL = 31
SHORT_KERNEL = 3
D_FF = 4 * D_MODEL
N_MOD = 6
RMS_EPS = 1e-6
LN_EPS = 1e-5

kernel_name = "hybrid_conformer_shortconv_adaln_trunk"


def rmsnorm(x, g):
    xf = x.astype(jnp.float32)
    y = xf * lax.rsqrt(jnp.mean(xf * xf, axis=-1, keepdims=True) + RMS_EPS)
    return (y * g.astype(jnp.float32)).astype(x.dtype)


def layernorm(x, g, b):
    xf = x.astype(jnp.float32)
    mu = jnp.mean(xf, axis=-1, keepdims=True)
    var = jnp.mean(jnp.square(xf - mu), axis=-1, keepdims=True)
    y = (xf - mu) * lax.rsqrt(var + LN_EPS)
    return (y * g.astype(jnp.float32) + b.astype(jnp.float32)).astype(x.dtype)


def causal_depthwise_conv(u, w):
    k = w.shape[0]
    return lax.conv_general_dilated(
        u, w[:, None, :].astype(u.dtype),
        window_strides=(1,), padding=((k - 1, 0),),
        dimension_numbers=("NWC", "WIO", "NWC"),
        feature_group_count=u.shape[-1])


def conformer_conv_module(h, w1, b1, dw, dwb, ln_g, ln_b, w2, b2):
    u = jnp.einsum("bsd,de->bse", h, w1) + b1
    a, g = jnp.split(u, 2, axis=-1)
    u = a * jax.nn.sigmoid(g)
    u = causal_depthwise_conv(u, dw) + dwb
    u = jax.nn.silu(layernorm(u, ln_g, ln_b))
    return jnp.einsum("bsd,de->bse", u, w2) + b2


def short_gated_conv(h, w_in, w_conv, w_out):
    z = jnp.einsum("bsd,de->bse", h, w_in)
    gate_b, gate_c, xin = jnp.split(z, 3, axis=-1)
    u = causal_depthwise_conv(gate_c * xin, w_conv)
    return jnp.einsum("bsd,de->bse", gate_b * u, w_out)


def sq_relu_mlp(h, w1, w2):
    u = jnp.square(jax.nn.relu(jnp.einsum("bsd,df->bsf", h, w1)))
    return jnp.einsum("bsf,fd->bsd", u, w2)


def setup_inputs(seed: int = 0) -> dict:
    key = jax.random.key(seed)
    ks = jax.random.split(key, 24)
    D = D_MODEL
    f32 = jnp.float32

    def nrm(k, shape, scale):
        return jax.random.normal(k, shape, f32) * scale

    def gain(k, shape):
        return 1.0 + 0.05 * jax.random.normal(k, shape, f32)

    return {
        "x": nrm(ks[0], (BATCH, SEQ, D), 1.0),
        "c": nrm(ks[1], (BATCH, D), 1.0),
        "mod_w": nrm(ks[2], (DEPTH, D, N_MOD * D), 0.5 * D ** -0.5),
        "mod_b": nrm(ks[3], (DEPTH, N_MOD * D), 0.02),
        "pre_mix_g": gain(ks[4], (DEPTH, D)),
        "post_mix_g": gain(ks[5], (DEPTH, D)),
        "pre_ffn_g": gain(ks[6], (DEPTH, D)),
        "post_ffn_g": gain(ks[7], (DEPTH, D)),
        "a_w1": nrm(ks[8], (N_A, D, 2 * D), D ** -0.5),
        "a_b1": nrm(ks[9], (N_A, 2 * D), 0.02),
        "a_dw": nrm(ks[10], (N_A, CONF_KERNEL, D), CONF_KERNEL ** -0.5),
        "a_dwb": nrm(ks[11], (N_A, D), 0.02),
        "a_ln_g": gain(ks[12], (N_A, D)),
        "a_ln_b": nrm(ks[13], (N_A, D), 0.02),
        "a_w2": nrm(ks[14], (N_A, D, D), D ** -0.5),
        "a_b2": nrm(ks[15], (N_A, D), 0.02),
        "b_w_in": nrm(ks[16], (N_B, D, 3 * D), D ** -0.5),
        "b_conv": nrm(ks[17], (N_B, SHORT_KERNEL, D), SHORT_KERNEL ** -0.5),
        "b_w_out": nrm(ks[18], (N_B, D, D), D ** -0.5),
        "f_w1": nrm(ks[19], (DEPTH, D, D_FF), D ** -0.5),
        "f_w2": nrm(ks[20], (DEPTH, D_FF, D), D_FF ** -0.5),
    }


def reference(x, c, mod_w, mod_b, pre_mix_g, post_mix_g, pre_ffn_g, post_ffn_g,
              a_w1, a_b1, a_dw, a_dwb, a_ln_g, a_ln_b, a_w2, a_b2,
              b_w_in, b_conv, b_w_out, f_w1, f_w2):
    c_act = jax.nn.silu(c)
    for i in range(DEPTH):
        mod = jnp.einsum("bd,de->be", c_act, mod_w[i]) + mod_b[i]
        sh_m, sc_m, gt_m, sh_f, sc_f, gt_f = [m[:, None, :] for m in jnp.split(mod, N_MOD, axis=-1)]

        h = rmsnorm(x, pre_mix_g[i]) * (1.0 + sc_m) + sh_m
        j = i // N_MIXERS
        if i % N_MIXERS == 0:
            y = conformer_conv_module(h, a_w1[j], a_b1[j], a_dw[j], a_dwb[j],
                                      a_ln_g[j], a_ln_b[j], a_w2[j], a_b2[j])
        else:
            y = short_gated_conv(h, b_w_in[j], b_conv[j], b_w_out[j])
        x = x + gt_m * rmsnorm(y, post_mix_g[i])

        h = rmsnorm(x, pre_ffn_g[i]) * (1.0 + sc_f) + sh_f
        y = sq_relu_mlp(h, f_w1[i], f_w2[i])
        x = x + gt_f * rmsnorm(y, post_ffn_g[i])
    return x
```

```python
import numpy as np
from contextlib import ExitStack
import concourse.bass as bass
import concourse.mybir as mybir
from concourse.bass_utils import run_bass_kernel_spmd

F32 = mybir.dt.float32
BF16 = mybir.dt.bfloat16
AF = mybir.ActivationFunctionType
ALU = mybir.AluOpType

D = 2048
NCH = 16
DFF = 8192
DEPTH = 4
SEQ = 4096
BATCH = 4
TW = 352
TWS = [352, 344, 344]
NT = len(TWS)
TOFF = [0, 352, 696, 1040]
TB = TOFF[-1]
NBLK = 2
TC = TB * NBLK
FB = 1024
NFB = DFF // FB
KFB = FB // 128
RING = 4
TFW = TB + 32
RMS_EPS = 1e-6
LN_EPS = 1e-5
CONF_K = 31
N_CORES = 8
MAINB = [0, 1, 2, 3]
STATB = [4, 5, 6]
MODB = 7
NPE = 14


def _param_layout():
    off = {}
    n = 0
    for l in range(DEPTH):
        for name, w in (("modb", 96), ("pre_mix_g", 16), ("post_mix_g", 16), ("pre_ffn_g", 16), ("post_ffn_g", 16)):
            off[(l, name)] = n
            n += w
        if l % 2 == 0:
            for name, w in (("a_b1", 32), ("a_dw", CONF_K * 16), ("a_dwb", 16), ("a_ln_g", 16), ("a_ln_b", 16), ("a_b2", 16)):
                off[(l, name)] = n
                n += w
        else:
            off[(l, "b_conv")] = n
            n += 48
    return off, n


POFF, NPRM = _param_layout()


def _vec_cols(v):
    v = np.asarray(v, dtype=np.float32)
    return np.ascontiguousarray(v.reshape(-1, 128).T)


def _pack_params(inp):
    prm = np.zeros((128, NPRM), dtype=np.float32)

    def put(l, name, arr):
        a = _vec_cols(arr)
        prm[:, POFF[(l, name)]:POFF[(l, name)] + a.shape[1]] = a

    for l in range(DEPTH):
        put(l, "modb", inp["mod_b"][l])
        for nm in ("pre_mix_g", "post_mix_g", "pre_ffn_g", "post_ffn_g"):
            put(l, nm, inp[nm][l])
        j = l // 2
        if l % 2 == 0:
            put(l, "a_b1", inp["a_b1"][j])
            put(l, "a_dw", inp["a_dw"][j].reshape(-1))
            put(l, "a_dwb", inp["a_dwb"][j])
            put(l, "a_ln_g", inp["a_ln_g"][j])
            put(l, "a_ln_b", inp["a_ln_b"][j])
            put(l, "a_b2", inp["a_b2"][j])
        else:
            put(l, "b_conv", inp["b_conv"][j].reshape(-1))
    return prm


class Res:
    __slots__ = ("name", "w", "r")

    def __init__(self, name):
        self.name = name
        self.w = None
        self.r = {}


def _flat(x):
    out = []
    for e in x:
        if isinstance(e, (list, tuple)):
            out.extend(_flat(e))
        else:
            out.append(e)
    return out


class Trk:
    def __init__(self, nc, stack, dry=False):
        self.nc = nc
        self.dry = dry
        self.stack = stack
        self.eng = {"pe": nc.tensor, "act": nc.scalar, "dve": nc.vector, "pool": nc.gpsimd, "sp": nc.sync}
        self.sem = {}
        if not dry:
            for k in self.eng:
                self.sem[k] = stack.enter_context(nc.semaphore("s_" + k))
        self.cnt = {k: 0 for k in self.eng}
        self.waited = {k: {} for k in self.eng}
        self.nwaits = 0
        self.nops = 0
        self.self_sync = {"pe": False, "act": True, "dve": True, "pool": True, "sp": True}

    def dsem(self, name):
        if self.dry:
            return [None, 0]
        return [self.stack.enter_context(self.nc.semaphore(name)), 0]

    def _wait(self, e, sem, val):
        key = id(sem)
        if self.waited[e].get(key, 0) >= val:
            return
        self.eng[e].wait_ge(sem, val)
        self.waited[e][key] = val
        self.nwaits += 1

    def op(self, e, fn, reads=(), writes=(), dsem=None, ss=True):
        if self.dry:
            return None
        reads = _flat(reads)
        writes = _flat(writes)
        deps = []
        for r in reads:
            if r.w is not None:
                deps.append(r.w)
        for r in writes:
            if r.w is not None:
                deps.append(r.w)
            deps.extend(r.r.values())
        for (de, sem, val) in deps:
            if de == e and (not ss or not self.self_sync[e]):
                continue
            self._wait(e, sem, val)
        ins = fn()
        self.nops += 1
        if dsem is not None:
            for i_ in (ins if isinstance(ins, list) else [ins]):
                dsem[1] += 16
                i_.then_inc(dsem[0], 16)
            me = ("dma", dsem[0], dsem[1])
        else:
            self.cnt[e] += 1
            ins.then_inc(self.sem[e], 1)
            me = (e, self.sem[e], self.cnt[e])
        for r in reads:
            r.r[(me[0], id(me[1]))] = me
        for r in writes:
            r.w = me
            r.r = {}
        return me

    def wait_for(self, e, dep):
        if self.dry or dep is None:
            return
        self._wait(e, dep[1], dep[2])


def build_nc(layers=tuple(range(DEPTH)), first=True):
    nc = bass.Bass("TRN2", target_bir_lowering=False)
    xT = nc.dram_tensor("xT", [D, TC], F32, kind="ExternalInput").ap()
    cT = nc.dram_tensor("cT", [128, NCH], F32, kind="ExternalInput").ap()
    prm_d = nc.dram_tensor("prm", [128, NPRM], F32, kind="ExternalInput").ap()
    mod_w = nc.dram_tensor("mod_w", [DEPTH, D, 6 * D], F32, kind="ExternalInput").ap()
    a_w1 = nc.dram_tensor("a_w1", [2, D, 2 * D], F32, kind="ExternalInput").ap()
    a_w2 = nc.dram_tensor("a_w2", [2, D, D], F32, kind="ExternalInput").ap()
    b_w_in = nc.dram_tensor("b_w_in", [2, D, 3 * D], F32, kind="ExternalInput").ap()
    b_w_out = nc.dram_tensor("b_w_out", [2, D, D], F32, kind="ExternalInput").ap()
    f_w1 = nc.dram_tensor("f_w1", [DEPTH, D, DFF], F32, kind="ExternalInput").ap()
    f_w2 = nc.dram_tensor("f_w2", [DEPTH, DFF, D], F32, kind="ExternalInput").ap()
    ident_d = nc.dram_tensor("ident", [128, 128], F32, kind="ExternalInput").ap()
    outT = nc.dram_tensor("outT", [D, TC], F32, kind="ExternalOutput").ap()

    with ExitStack() as st_:
        sb = lambda name, shape, dt: st_.enter_context(nc.sbuf_tensor(name, shape, dt))
        ring = sb("ring", [128, RING, 16, 2, 128], BF16)
        hb = sb("hb", [128, NCH, TB], BF16)
        ub = sb("ub", [128, KFB, TB], BF16)
        BIG = sb("BIG", [128, NCH, TB], F32)
        tf = sb("tf", [128, 4, TFW], F32)
        sqb = sb("sqb", [128, 4, TB], BF16)
        stt = sb("stt", [128, 6, TW], F32)
        v16 = sb("v16", [128, 3, TFW], BF16)
        dg = sb("dg", [128, 2, NPE, 128], BF16)
        ident = sb("ident_s", [128, 128], F32)
        hv = sb("hv", [128, NCH, 32], F32)
        hp = sb("hp", [128, NCH, 2], F32)
        prm = sb("prm_s", [128, NPRM], F32)
        modT = sb("modT", [128, 2, 96], F32)
        der = sb("der", [128, 2, 64], F32)
        cab = sb("cab", [128, NCH], BF16)
        cTf = sb("cTf", [128, NCH], F32)
        ones = sb("ones", [128, 128], BF16)
        epsr = sb("epsr", [128, 2], F32)
        ps = st_.enter_context(nc.psum_tensor("ps", [128, 8, 512], F32))

        def emit(T, W):
            R_ring = [Res("ring%d" % i) for i in range(RING)]
            S_ring = [T.dsem("dring%d" % i) for i in range(RING)]
            R_hb = [[Res("hb") for t in range(NT)] for c in range(NCH)]
            R_ub = [[Res("ub") for t in range(NT)] for k in range(KFB)]
            R_big = [[Res("big") for t in range(NT)] for c in range(NCH)]
            S_big = [T.dsem("dbig%d" % c) for c in range(NCH)]
            R_tf = [[Res("tf%d" % i)] + ([Res("tft") for t in range(NT)] if i >= 2 else []) for i in range(4)]
            S_tf = [T.dsem("dtf%d" % i) for i in range(4)]
            R_sqb = [Res("sqb%d" % i) for i in range(4)]
            R_st = [Res("st%d" % i) for i in range(6)]
            R_v16 = [Res("v16_0"), Res("v16_1"), Res("v16_2")]
            R_dg = [Res("dg0"), Res("dg1")]
            R_ident = Res("ident")
            S_ident = T.dsem("dident")
            R_bank = [Res("bank%d" % i) for i in range(8)]
            R_hv = [Res("hv") for c in range(NCH)]
            R_hp = [Res("hp") for c in range(NCH)]
            R_prm = Res("prm")
            S_prm = T.dsem("dprm")
            R_cT = Res("cT")
            S_cT = T.dsem("dcT")
            R_cab = Res("cab")
            R_ones = Res("ones")
            R_modT = [Res("modT0"), Res("modT1")]
            R_der = [Res("der0"), Res("der1")]
            R_x = [[Res("x") for b in range(NBLK)] for c in range(NCH)]
            S_x = [[T.dsem("dx%d_%d" % (c, b)) for b in range(NBLK)] for c in range(NCH)]
            state = {"bank": 0, "sq": 0, "st": 0}

            def nbank():
                b = MAINB[state["bank"] % len(MAINB)]
                state["bank"] += 1
                return b

            def nsq():
                i = state["sq"] % 4
                state["sq"] += 1
                return i

            def nst():
                i = state["st"] % 6
                state["st"] += 1
                return i

            def P(l, name, lo, n=1):
                o = POFF[(l, name)] + lo
                return prm[:, o:o + n]

            tok = lambda t: slice(TOFF[t], TOFF[t + 1])
            rows = lambda c: slice(c * 128, (c + 1) * 128)
            cols = lambda b: slice(b * TB, (b + 1) * TB)

            def lhs16(s, c, jj):
                return ring[:, s, c, jj, :]

            def lhs8(s, k, jj):
                return ring[:, s, 2 * k + jj // 2, jj % 2, :]

            def issue_unit(desc, s):
                kind = desc[0]
                dst = ring[:, s]
                if kind == "mod":
                    _, l, u = desc
                    src = mod_w[l, :, u * 256:(u + 1) * 256].rearrange("(c p) (h n) -> p c h n", p=128, h=2)
                elif kind == "a1":
                    _, la, j = desc
                    src = [a_w1[la, :, h * D + j * 128:h * D + (j + 1) * 128].rearrange("(c p) n -> p c n", p=128) for h in range(2)]
                elif kind == "a2":
                    _, la, eu = desc
                    src = a_w2[la, :, eu * 256:(eu + 1) * 256].rearrange("(c p) (h n) -> p c h n", p=128, h=2)
                elif kind == "bcx":
                    _, lb, j = desc
                    src = [b_w_in[lb, :, (h + 1) * D + j * 128:(h + 1) * D + (j + 1) * 128].rearrange("(c p) n -> p c n", p=128) for h in range(2)]
                elif kind == "bb":
                    _, lb, eu = desc
                    src = b_w_in[lb, :, eu * 256:(eu + 1) * 256].rearrange("(c p) (h n) -> p c h n", p=128, h=2)
                elif kind == "bo":
                    _, lb, eu = desc
                    src = b_w_out[lb, :, eu * 256:(eu + 1) * 256].rearrange("(c p) (h n) -> p c h n", p=128, h=2)
                elif kind == "f1":
                    _, l, fb, u = desc
                    c0 = fb * FB + u * 256
                    src = f_w1[l, :, c0:c0 + 256].rearrange("(c p) (h n) -> p c h n", p=128, h=2)
                elif kind == "f2":
                    _, l, fb, u = desc
                    src = f_w2[l, fb * FB:(fb + 1) * FB, u * 512:(u + 1) * 512].rearrange(
                        "(k p) n -> p k n", p=128)
                else:
                    raise ValueError(kind)
                if isinstance(src, list):
                    T.op("pool", lambda: [nc.gpsimd.dma_start(out=ring[:, s, :, h, :], in_=src[h]) for h in range(2)],
                         writes=[R_ring[s]], dsem=S_ring[s])
                else:
                    T.op("pool", lambda: nc.gpsimd.dma_start(out=dst, in_=src), writes=[R_ring[s]], dsem=S_ring[s])

            W.issue = issue_unit

            def group(out_ap, lhs_fn, rhs_fn, n):
                ins = None
                for c in range(n):
                    ins = nc.tensor.matmul(out_ap, lhsT=lhs_fn(c), rhs=rhs_fn(c), start=(c == 0), stop=(c == n - 1))
                return ins

            def setup():
                T.op("sp", lambda: nc.sync.dma_start(out=prm[:], in_=prm_d), writes=[R_prm], dsem=S_prm)
                T.op("sp", lambda: nc.sync.dma_start(out=cTf[:], in_=cT), writes=[R_cT], dsem=S_cT)
                T.op("sp", lambda: nc.sync.dma_start(out=ident[:], in_=ident_d), writes=[R_ident], dsem=S_ident)
                T.op("dve", lambda: nc.vector.memset(ones[:], 1.0), writes=[R_ones])
                T.op("dve", lambda: nc.vector.memset(epsr[:, 0:1], RMS_EPS), writes=[R_ones])
                T.op("dve", lambda: nc.vector.memset(epsr[:, 1:2], LN_EPS), writes=[R_ones])
                T.op("act", lambda: nc.scalar.activation(out=cab[:], in_=cTf[:], func=AF.Silu), reads=[R_cT], writes=[R_cab])

            def mod_unit(l, u):
                s = W.get(("mod", l, u))

                def f():
                    ins = None
                    for jj in range(2):
                        col = u * 2 + jj
                        ins = group(ps[:, MODB, col:col + 1], lambda c: lhs16(s, c, jj), lambda c: cab[:, c:c + 1], NCH)
                    return ins
                T.op("pe", f, reads=[R_ring[s], R_cab], writes=[R_bank[MODB]])

            MOD_PIECES = [(0, 32, 16, (0, 16, "pre_mix_g", True)), (32, 48, 24, (1, 32, "post_mix_g", False)),
                          (48, 80, 40, (2, 64, "pre_ffn_g", True)), (80, 96, 48, (3, 80, "post_ffn_g", False))]

            def mod_piece(l, pi):
                p = l % 2
                c0, c1, _, (i, sc0, gname, is_gs) = MOD_PIECES[pi]
                T.op("dve", lambda: nc.vector.tensor_tensor(out=modT[:, p, c0:c1], in0=ps[:, MODB, c0:c1], in1=P(l, "modb", c0, c1 - c0), op=ALU.add),
                     reads=[R_bank[MODB], R_prm], writes=[R_modT[p]])
                o = der[:, p, i * 16:(i + 1) * 16]
                if is_gs:
                    fn = lambda: nc.vector.scalar_tensor_tensor(out=o, in0=modT[:, p, sc0:sc0 + 16], scalar=1.0, in1=P(l, gname, 0, 16),
                                                                op0=ALU.add, op1=ALU.mult)
                else:
                    fn = lambda: nc.vector.tensor_tensor(out=o, in0=modT[:, p, sc0:sc0 + 16], in1=P(l, gname, 0, 16), op=ALU.mult)
                T.op("dve", fn, reads=[R_modT[p], R_prm], writes=[R_der[p]])

            def mod_finish(l):
                for pi in range(4):
                    mod_piece(l, pi)

            lazy = {"l": None, "next_u": 48, "next_piece": 4}

            def mod_tick():
                if lazy["l"] is None or lazy["next_u"] >= 48:
                    return
                mod_unit(lazy["l"], lazy["next_u"])
                lazy["next_u"] += 1
                while lazy["next_piece"] < 4 and MOD_PIECES[lazy["next_piece"]][2] <= lazy["next_u"]:
                    mod_piece(lazy["l"], lazy["next_piece"])
                    lazy["next_piece"] += 1

            def mod_need(pi):
                while lazy["l"] is not None and lazy["next_piece"] <= pi:
                    mod_tick()

            S_fb = [T.dsem("dfb%d" % i) for i in range(14)]

            def sbuf_f32(i):
                if i < 8:
                    ap = hb[:, 2 * i:2 * i + 2, :].rearrange("p a b -> p (a b)").bitcast(F32)
                    res = [R_hb[2 * i], R_hb[2 * i + 1]]
                elif i < 12:
                    j = i - 8
                    ap = ub[:, 2 * j:2 * j + 2, :].rearrange("p a b -> p (a b)").bitcast(F32)
                    res = R_ub[2 * j] + R_ub[2 * j + 1]
                else:
                    ap = tf[:, i - 12, 0:TB]
                    res = [R_tf[i - 12]]
                return ap, res, S_fb[i]

            def stat_chunk(c):
                i = nsq()
                T.op("act", lambda: nc.scalar.activation(out=sqb[:, i, :], in_=BIG[:, c, :], func=AF.Square),
                     reads=R_big[c], writes=[R_sqb[i]])

                def f():
                    ins = None
                    for t in range(NT):
                        ins = nc.tensor.matmul(ps[:, STATB[t], 0:TWS[t]], lhsT=ones[:], rhs=sqb[:, i, tok(t)],
                                               start=(c == 0), stop=(c == NCH - 1))
                    return ins
                T.op("pe", f, reads=[R_sqb[i], R_ones], writes=[R_bank[b] for b in STATB])

            STAT_LAG = 2
            pend = []

            def stat_lag(c):
                pend.append(c)
                if len(pend) > STAT_LAG:
                    stat_chunk(pend.pop(0))

            def stat_flush():
                while pend:
                    stat_chunk(pend.pop(0))

            def rstd_finish():
                stat_flush()
                for t in range(NT):
                    T.op("act", lambda: nc.scalar.activation(out=tf[:, 3, tok(t)], in_=ps[:, STATB[t], 0:TWS[t]], func=AF.Sqrt,
                                                             scale=1.0 / D, bias=epsr[:, 0:1]),
                         reads=[R_bank[STATB[t]], R_ones], writes=[R_tf[3][1 + t]])
                for t in range(NT):
                    T.op("dve", lambda: nc.vector.reciprocal(out=tf[:, 2, tok(t)], in_=tf[:, 3, tok(t)]),
                         reads=[R_tf[3][1 + t]], writes=[R_tf[2][1 + t]])

            def prenorm(src, blk, gs, sh, rd_gs, rd_sh, mode):
                if mode == "load":
                    for c in range(NCH):
                        T.op("sp", lambda: nc.sync.dma_start(out=BIG[:, c, :], in_=src[rows(c), cols(blk)]),
                             reads=[R_x[c][blk]], writes=R_big[c], dsem=S_big[c])
                if mode != "stats":
                    for c in range(NCH):
                        stat_chunk(c)
                rstd_finish()
                for t in range(NT):
                    for c in range(NCH):
                        r = c % 2
                        T.op("dve", lambda: nc.vector.tensor_tensor(out=tf[:, r, tok(t)], in0=BIG[:, c, tok(t)], in1=tf[:, 2, tok(t)], op=ALU.mult),
                             reads=[R_big[c][t], R_tf[2][1 + t]], writes=[R_tf[r]])
                        T.op("act", lambda: nc.scalar.activation(out=hb[:, c, tok(t)], in_=tf[:, r, tok(t)], func=AF.Identity,
                                                                 scale=gs[:, c:c + 1], bias=sh[:, c:c + 1]),
                             reads=[R_tf[r], rd_gs, rd_sh], writes=[R_hb[c][t]])

            def postnorm_residual(src, blk, gp, rd_gp, keep, next_load=None):
                mod_need(1 if keep else 3)
                rstd_finish()
                if keep:
                    kb = [sbuf_f32(8 + i) for i in range(4)] + [sbuf_f32(12), sbuf_f32(13)]
                    nb = len(kb)

                    def xload(c):
                        ap, res, ds = kb[c % nb]
                        T.op("sp", lambda: nc.sync.dma_start(out=ap, in_=src[rows(c), cols(blk)]),
                             reads=[R_x[c][blk]], writes=res, dsem=ds)
                    for c in range(nb):
                        xload(c)
                    for c in range(NCH):
                        ap, res, ds = kb[c % nb]
                        T.op("dve", lambda: nc.vector.scalar_tensor_tensor(out=BIG[:, c, :], in0=BIG[:, c, :], scalar=gp[:, c:c + 1],
                                                                          in1=tf[:, 2, 0:TB], op0=ALU.mult, op1=ALU.mult),
                             reads=R_big[c] + [R_tf[2], rd_gp], writes=R_big[c])
                        en, E = ("pool", nc.gpsimd) if c % 3 == 1 else ("dve", nc.vector)
                        T.op(en, lambda: E.tensor_tensor(out=BIG[:, c, :], in0=BIG[:, c, :], in1=ap, op=ALU.add),
                             reads=R_big[c] + res, writes=R_big[c])
                        T.op("sp", lambda: nc.sync.dma_start(out=outT[rows(c), cols(blk)], in_=BIG[:, c, :]),
                             reads=R_big[c], writes=[R_x[c][blk]], dsem=S_x[c][blk])
                        if c + nb < NCH:
                            xload(c + nb)
                        stat_chunk(c)
                else:
                    for c in range(NCH):
                        ap, res, _ = sbuf_f32(c % 14)
                        T.op("dve", lambda: nc.vector.scalar_tensor_tensor(out=ap, in0=BIG[:, c, :], scalar=gp[:, c:c + 1],
                                                                          in1=tf[:, 2, 0:TB], op0=ALU.mult, op1=ALU.mult),
                             reads=R_big[c] + [R_tf[2], rd_gp], writes=res)
                        T.op("pool", lambda: nc.gpsimd.dma_start(out=outT[rows(c), cols(blk)], in_=ap, accum_op=ALU.add),
                             reads=res, writes=[R_x[c][blk]], dsem=S_x[c][blk])
                        if next_load is not None:
                            next_load(c)

            def make_next_load(nsrc, nblk):
                def nl(c):
                    T.op("sp", lambda: nc.sync.dma_start(out=BIG[:, c, :], in_=nsrc[rows(c), cols(nblk)]),
                         reads=[R_x[c][nblk]], writes=R_big[c], dsem=S_big[c])
                return nl

            def mixer_a(l, blk, src, preloaded):
                la = l // 2
                p = l % 2
                prenorm(src, blk, der[:, p, 0:16], modT[:, p, 0:16], R_der[p], R_modT[p], "loaded" if preloaded else "load")
                def a_glu(j):
                    vq = j % 3
                    dq = j % 2
                    for k in range(NPE):
                        T.op("act", lambda: nc.scalar.activation(out=dg[:, dq, k, :], in_=ident[:], func=AF.Identity,
                                                                 scale=P(l, "a_dw", k * 16 + j)),
                             reads=[R_ident, R_prm], writes=[R_dg[dq]], ss=(k == 0))
                    mod_tick()
                    s = W.get(("a1", la, j))
                    if blk == 0:
                        T.op("pool", lambda: nc.gpsimd.memset(v16[:, vq, 0:32], 0.0), writes=[R_v16[vq]])
                    else:
                        T.op("pool", lambda: nc.gpsimd.tensor_copy(out=v16[:, vq, 2:32], in_=hv[:, j, 0:30]),
                             reads=[R_hv[j]], writes=[R_v16[vq]])
                    for t in range(NT):
                        ba = nbank()
                        T.op("pe", lambda: group(ps[:, ba, 0:TWS[t]], lambda c: ring[:, s, c, 0, :], lambda c: hb[:, c, tok(t)], NCH),
                             reads=[R_ring[s]] + [R_hb[c_][t] for c_ in range(NCH)], writes=[R_bank[ba]])
                        bg = nbank()
                        T.op("pe", lambda: group(ps[:, bg, 0:TWS[t]], lambda c: ring[:, s, c, 1, :], lambda c: hb[:, c, tok(t)], NCH),
                             reads=[R_ring[s]] + [R_hb[c_][t] for c_ in range(NCH)], writes=[R_bank[bg]])
                        ka = nst()
                        T.op("act", lambda: nc.scalar.activation(out=stt[:, ka, 0:TWS[t]], in_=ps[:, ba, 0:TWS[t]], func=AF.Identity,
                                                                 bias=P(l, "a_b1", j), scale=1.0),
                             reads=[R_bank[ba], R_prm], writes=[R_st[ka]])
                        ks = nst()
                        T.op("act", lambda: nc.scalar.activation(out=stt[:, ks, 0:TWS[t]], in_=ps[:, bg, 0:TWS[t]], func=AF.Sigmoid,
                                                                 bias=P(l, "a_b1", 16 + j), scale=1.0),
                             reads=[R_bank[bg], R_prm], writes=[R_st[ks]])
                        T.op("pool", lambda: nc.gpsimd.tensor_tensor(out=v16[:, vq, 32 + TOFF[t]:32 + TOFF[t + 1]], in0=stt[:, ka, 0:TWS[t]],
                                                                    in1=stt[:, ks, 0:TWS[t]], op=ALU.mult),
                             reads=[R_st[ka], R_st[ks]], writes=[R_v16[vq]])
                    if blk < NBLK - 1:
                        T.op("act", lambda: nc.scalar.copy(out=hv[:, j, 0:30], in_=v16[:, vq, 32 + TB - 30:32 + TB]),
                             reads=[R_v16[vq]], writes=[R_hv[j]])

                def a_conv(j):
                    vq = j % 3
                    dq = j % 2
                    for t in range(NT):
                        b = nbank()
                        T.op("pe", lambda: group(ps[:, b, 0:TWS[t]], lambda k: dg[:, dq, k, :],
                                                 lambda k: v16[:, vq, 2 + k + TOFF[t]:2 + k + TOFF[t + 1]], NPE),
                             reads=[R_dg[dq], R_v16[vq]], writes=[R_bank[b]])
                        T.op("act", lambda: nc.scalar.activation(out=BIG[:, j, tok(t)], in_=ps[:, b, 0:TWS[t]], func=AF.Identity,
                                                                 bias=P(l, "a_dwb", j), scale=1.0),
                             reads=[R_bank[b], R_prm], writes=[R_big[j][t]])
                    for k in range(NPE, CONF_K):
                        T.op("dve", lambda: nc.vector.scalar_tensor_tensor(out=BIG[:, j, :], in0=v16[:, vq, 2 + k:2 + k + TB],
                                                                          scalar=P(l, "a_dw", k * 16 + j), in1=BIG[:, j, :],
                                                                          op0=ALU.mult, op1=ALU.add),
                             reads=[R_v16[vq], R_prm] + R_big[j], writes=R_big[j], ss=(k == NPE))

                for j in range(NCH + 1):
                    if j < NCH:
                        a_glu(j)
                    if j >= 1:
                        a_conv(j - 1)
                QB = [1, 2, 3]
                for j in range(NCH):
                    i0 = nsq()
                    i1 = nsq()
                    T.op("act", lambda: nc.scalar.copy(out=sqb[:, i0, :], in_=BIG[:, j, :]), reads=R_big[j], writes=[R_sqb[i0]])
                    T.op("act", lambda: nc.scalar.activation(out=sqb[:, i1, :], in_=BIG[:, j, :], func=AF.Square),
                         reads=R_big[j], writes=[R_sqb[i1]])

                    def f():
                        ins = None
                        for t in range(NT):
                            nc.tensor.matmul(ps[:, STATB[t], 0:TWS[t]], lhsT=ones[:], rhs=sqb[:, i0, tok(t)], start=(j == 0), stop=(j == NCH - 1))
                            ins = nc.tensor.matmul(ps[:, QB[t], 0:TWS[t]], lhsT=ones[:], rhs=sqb[:, i1, tok(t)], start=(j == 0), stop=(j == NCH - 1))
                        return ins
                    T.op("pe", f, reads=[R_sqb[i0], R_sqb[i1], R_ones], writes=[R_bank[b] for b in STATB + QB])
                for t in range(NT):
                    T.op("dve", lambda: nc.vector.tensor_scalar(out=tf[:, 3, tok(t)], in0=ps[:, STATB[t], 0:TWS[t]], scalar1=1.0 / D, scalar2=None, op0=ALU.mult),
                         reads=[R_bank[STATB[t]]], writes=[R_tf[3]])
                T.op("dve", lambda: nc.vector.tensor_tensor(out=tf[:, 0, 0:TB], in0=tf[:, 3, 0:TB], in1=tf[:, 3, 0:TB], op=ALU.mult),
                     reads=[R_tf[3]], writes=[R_tf[0]])
                for t in range(NT):
                    T.op("dve", lambda: nc.vector.scalar_tensor_tensor(out=tf[:, 0, tok(t)], in0=ps[:, QB[t], 0:TWS[t]], scalar=1.0 / D,
                                                                      in1=tf[:, 0, tok(t)], op0=ALU.mult, op1=ALU.subtract),
                         reads=[R_bank[QB[t]], R_tf[0]], writes=[R_tf[0]])
                T.op("act", lambda: nc.scalar.activation(out=tf[:, 0, 0:TB], in_=tf[:, 0, 0:TB], func=AF.Sqrt, scale=1.0, bias=epsr[:, 1:2]),
                     reads=[R_tf[0], R_ones], writes=[R_tf[0]])
                T.op("dve", lambda: nc.vector.reciprocal(out=tf[:, 2, 0:TB], in_=tf[:, 0, 0:TB]), reads=[R_tf[0]], writes=[R_tf[2]])
                T.op("dve", lambda: nc.vector.scalar_tensor_tensor(out=tf[:, 3, 0:TB], in0=tf[:, 3, 0:TB], scalar=-1.0, in1=tf[:, 2, 0:TB],
                                                                  op0=ALU.mult, op1=ALU.mult),
                     reads=[R_tf[3], R_tf[2]], writes=[R_tf[3]])
                for j in range(NCH):
                    r = j % 2
                    T.op("dve", lambda: nc.vector.tensor_tensor(out=tf[:, r, 0:TB], in0=BIG[:, j, :], in1=tf[:, 2, 0:TB], op=ALU.mult),
                         reads=R_big[j] + [R_tf[2]], writes=[R_tf[r]])
                    T.op("dve", lambda: nc.vector.tensor_tensor(out=tf[:, r, 0:TB], in0=tf[:, r, 0:TB], in1=tf[:, 3, 0:TB], op=ALU.add),
                         reads=[R_tf[r], R_tf[3]], writes=[R_tf[r]])
                    T.op("act", lambda: nc.scalar.activation(out=hb[:, j, :], in_=tf[:, r, 0:TB], func=AF.Silu,
                                                             scale=P(l, "a_ln_g", j), bias=P(l, "a_ln_b", j)),
                         reads=[R_tf[r], R_prm], writes=[R_hb[j]])
                for eu in range(8):
                    mod_tick()
                    s = W.get(("a2", la, eu))
                    for jj in range(2):
                        e = eu * 2 + jj
                        for t in range(NT):
                            b = nbank()
                            T.op("pe", lambda: group(ps[:, b, 0:TWS[t]], lambda c: lhs16(s, c, jj), lambda c: hb[:, c, tok(t)], NCH),
                                 reads=[R_ring[s]] + [R_hb[c_][t] for c_ in range(NCH)], writes=[R_bank[b]])
                            T.op("act", lambda: nc.scalar.activation(out=BIG[:, e, tok(t)], in_=ps[:, b, 0:TWS[t]], func=AF.Identity,
                                                                     bias=P(l, "a_b2", e), scale=1.0),
                                 reads=[R_bank[b], R_prm], writes=[R_big[e][t]])
                        stat_lag(e)
                postnorm_residual(src, blk, der[:, p, 16:32], R_der[p], keep=True)

            def mixer_b(l, blk, src, preloaded):
                lb = l // 2
                p = l % 2
                prenorm(src, blk, der[:, p, 0:16], modT[:, p, 0:16], R_der[p], R_modT[p], "loaded" if preloaded else "load")
                for j in range(NCH):
                    mod_tick()
                    s = W.get(("bcx", lb, j))
                    vr = j % 2
                    if blk == 0:
                        T.op("dve", lambda: nc.vector.memset(tf[:, vr, 0:32], 0.0), writes=[R_tf[vr]])
                    else:
                        T.op("dve", lambda: nc.vector.tensor_copy(out=tf[:, vr, 30:32], in_=hp[:, j, 0:2]),
                             reads=[R_hp[j]], writes=[R_tf[vr]])
                    for t in range(NT):
                        bc = nbank()
                        T.op("pe", lambda: group(ps[:, bc, 0:TWS[t]], lambda c: ring[:, s, c, 0, :], lambda c: hb[:, c, tok(t)], NCH),
                             reads=[R_ring[s]] + [R_hb[c_][t] for c_ in range(NCH)], writes=[R_bank[bc]])
                        bx = nbank()
                        T.op("pe", lambda: group(ps[:, bx, 0:TWS[t]], lambda c: ring[:, s, c, 1, :], lambda c: hb[:, c, tok(t)], NCH),
                             reads=[R_ring[s]] + [R_hb[c_][t] for c_ in range(NCH)], writes=[R_bank[bx]])
                        k = nst()
                        T.op("act", lambda: nc.scalar.copy(out=stt[:, k, 0:TWS[t]], in_=ps[:, bc, 0:TWS[t]]), reads=[R_bank[bc]], writes=[R_st[k]])
                        T.op("dve", lambda: nc.vector.tensor_tensor(out=tf[:, vr, 32 + TOFF[t]:32 + TOFF[t + 1]], in0=ps[:, bx, 0:TWS[t]],
                                                                   in1=stt[:, k, 0:TWS[t]], op=ALU.mult),
                             reads=[R_bank[bx], R_st[k]], writes=[R_tf[vr]])
                    if blk < NBLK - 1:
                        T.op("act", lambda: nc.scalar.copy(out=hp[:, j, 0:2], in_=tf[:, vr, 32 + TB - 2:32 + TB]),
                             reads=[R_tf[vr]], writes=[R_hp[j]])
                    T.op("act", lambda: nc.scalar.activation(out=BIG[:, j, :], in_=tf[:, vr, 30:30 + TB], func=AF.Identity,
                                                             scale=P(l, "b_conv", 0 * 16 + j)),
                         reads=[R_tf[vr], R_prm], writes=R_big[j])
                    for k in range(1, 3):
                        T.op("dve", lambda: nc.vector.scalar_tensor_tensor(out=BIG[:, j, :], in0=tf[:, vr, 30 + k:30 + k + TB],
                                                                          scalar=P(l, "b_conv", k * 16 + j), in1=BIG[:, j, :],
                                                                          op0=ALU.mult, op1=ALU.add),
                             reads=[R_tf[vr], R_prm] + R_big[j], writes=R_big[j])
                for eu in range(8):
                    mod_tick()
                    s = W.get(("bb", lb, eu))
                    for jj in range(2):
                        e = eu * 2 + jj
                        for t in range(NT):
                            b = nbank()
                            T.op("pe", lambda: group(ps[:, b, 0:TWS[t]], lambda c: lhs16(s, c, jj), lambda c: hb[:, c, tok(t)], NCH),
                                 reads=[R_ring[s]] + [R_hb[c_][t] for c_ in range(NCH)], writes=[R_bank[b]])
                            T.op("dve", lambda: nc.vector.tensor_tensor(out=BIG[:, e, tok(t)], in0=ps[:, b, 0:TWS[t]], in1=BIG[:, e, tok(t)], op=ALU.mult),
                                 reads=[R_bank[b], R_big[e][t]], writes=[R_big[e][t]])
                for c in range(NCH):
                    T.op("act", lambda: nc.scalar.copy(out=hb[:, c, :], in_=BIG[:, c, :]), reads=R_big[c], writes=[R_hb[c]])
                for eu in range(8):
                    mod_tick()
                    s = W.get(("bo", lb, eu))
                    for jj in range(2):
                        e = eu * 2 + jj
                        for t in range(NT):
                            b = nbank()
                            T.op("pe", lambda: group(ps[:, b, 0:TWS[t]], lambda c: lhs16(s, c, jj), lambda c: hb[:, c, tok(t)], NCH),
                                 reads=[R_ring[s]] + [R_hb[c_][t] for c_ in range(NCH)], writes=[R_bank[b]])
                            T.op("act", lambda: nc.scalar.copy(out=BIG[:, e, tok(t)], in_=ps[:, b, 0:TWS[t]]),
                                 reads=[R_bank[b]], writes=[R_big[e][t]])
                        stat_lag(e)
                postnorm_residual(src, blk, der[:, p, 16:32], R_der[p], keep=True)

            def ffn(l, blk, next_mod, next_load):
                p = l % 2
                mod_need(2)
                prenorm(outT, blk, der[:, p, 32:48], modT[:, p, 48:64], R_der[p], R_modT[p], "stats")
                for fb in range(NFB):
                    for u in range(4):
                        mod_tick()
                        s = W.get(("f1", l, fb, u))
                        for jj in range(2):
                            kk = u * 2 + jj
                            for t in range(NT):
                                b = nbank()
                                T.op("pe", lambda: group(ps[:, b, 0:TWS[t]], lambda c: lhs16(s, c, jj), lambda c: hb[:, c, tok(t)], NCH),
                                     reads=[R_ring[s]] + [R_hb[c_][t] for c_ in range(NCH)], writes=[R_bank[b]])
                                i = nst()
                                T.op("act", lambda: nc.scalar.activation(out=stt[:, i, 0:TWS[t]], in_=ps[:, b, 0:TWS[t]], func=AF.Relu),
                                     reads=[R_bank[b]], writes=[R_st[i]])
                                T.op("dve", lambda: nc.vector.tensor_tensor(out=ub[:, kk, tok(t)], in0=stt[:, i, 0:TWS[t]], in1=stt[:, i, 0:TWS[t]], op=ALU.mult),
                                     reads=[R_st[i]], writes=[R_ub[kk][t]])
                        if next_mod is not None and u < 3:
                            mod_unit(next_mod, (blk * NFB + fb) * 3 + u)
                    for u in range(4):
                        mod_tick()
                        s = W.get(("f2", l, fb, u))
                        for jj in range(4):
                            e = u * 4 + jj
                            for t in range(NT):
                                b = nbank()
                                T.op("pe", lambda: group(ps[:, b, 0:TWS[t]], lambda k: lhs8(s, k, jj), lambda k: ub[:, k, tok(t)], KFB),
                                     reads=[R_ring[s]] + [R_ub[k][t] for k in range(KFB)], writes=[R_bank[b]])
                                if fb == 0:
                                    T.op("act", lambda: nc.scalar.copy(out=BIG[:, e, tok(t)], in_=ps[:, b, 0:TWS[t]]),
                                         reads=[R_bank[b]], writes=[R_big[e][t]])
                                else:
                                    T.op("dve", lambda: nc.vector.tensor_tensor(out=BIG[:, e, tok(t)], in0=ps[:, b, 0:TWS[t]], in1=BIG[:, e, tok(t)], op=ALU.add),
                                         reads=[R_bank[b], R_big[e][t]], writes=[R_big[e][t]])
                            if fb == NFB - 1:
                                stat_lag(e)
                postnorm_residual(outT, blk, der[:, p, 48:64], R_der[p], keep=False, next_load=next_load)

            setup()
            for li, l in enumerate(layers):
                if li == 0:
                    for u in range(16):
                        mod_unit(l, u)
                    mod_piece(l, 0)
                    lazy.update(l=l, next_u=16, next_piece=1)
                else:
                    mod_need(3)
                    mod_finish(l)
                nxt = layers[li + 1] if li + 1 < len(layers) else None
                for blk in range(NBLK):
                    src = xT if (first and li == 0) else outT
                    pre = state.get("preloaded", False)
                    if l % 2 == 0:
                        mixer_a(l, blk, src, pre)
                    else:
                        mixer_b(l, blk, src, pre)
                    if blk + 1 < NBLK:
                        nl = make_next_load(src, blk + 1)
                    elif li + 1 < len(layers):
                        nl = make_next_load(outT, 0)
                    else:
                        nl = None
                    state["preloaded"] = nl is not None
                    ffn(l, blk, nxt, nl)
            for c in range(NCH):
                for b in range(NBLK):
                    T.wait_for("sp", R_x[c][b].w)

        class WStream:
            def __init__(self, plan):
                self.plan = plan
                self.rec = []
                self.i = 0
                self.issued = 0
                self.issue = None

            def get(self, desc):
                if self.plan is None:
                    self.rec.append(desc)
                    return 0
                assert self.plan[self.i] == desc, (self.plan[self.i], desc)
                while self.issued < min(len(self.plan), self.i + RING):
                    self.issue(self.plan[self.issued], self.issued % RING)
                    self.issued += 1
                s = self.i % RING
                self.i += 1
                return s

        dryW = WStream(None)
        emit(Trk(nc, st_, dry=True), dryW)
        T = Trk(nc, st_)
        W = WStream(dryW.rec)
        emit(T, W)
        assert W.i == len(W.plan)
        build_nc.stats = dict(nops=T.nops, nwaits=T.nwaits, cnt=dict(T.cnt), units=len(W.plan))
    return nc


_WKEYS = ("mod_w", "a_w1", "a_w2", "b_w_in", "b_w_out", "f_w1", "f_w2")
LAYERS_PER_LAUNCH = DEPTH


def kernel(**inputs):
    inp = {k: np.asarray(v) for k, v in inputs.items()}
    x = inp["x"].astype(np.float32, copy=False)
    c = inp["c"].astype(np.float32, copy=False)
    prm = _pack_params(inp)
    wts = {k: np.ascontiguousarray(inp[k], dtype=np.float32) for k in _WKEYS}
    starts = [0, SEQ - TC]
    cur = []
    for k in range(N_CORES):
        b, h = k // 2, k % 2
        cur.append(np.ascontiguousarray(x[b, starts[h]:starts[h] + TC, :].T))
    for l0 in range(0, DEPTH, LAYERS_PER_LAUNCH):
        layers = tuple(range(l0, min(DEPTH, l0 + LAYERS_PER_LAUNCH)))
        nc = build_nc(layers=layers, first=True)
        in_maps = []
        for k in range(N_CORES):
            b = k // 2
            m = {"xT": cur[k], "cT": _vec_cols(c[b]), "prm": prm, "ident": np.eye(128, dtype=np.float32)}
            m.update(wts)
            in_maps.append(m)
        res = run_bass_kernel_spmd(nc, in_maps, core_ids=list(range(N_CORES)))
        cur = [np.asarray(res.results[k]["outT"]) for k in range(N_CORES)]
    out = np.empty((BATCH, SEQ, D), dtype=np.float32)
    for k in range(N_CORES):
        b, h = k // 2, k % 2
        if h == 0:
            out[b, 0:TC, :] = cur[k].T
        else:
            out[b, TC:SEQ, :] = cur[k][:, 2 * TC - SEQ:TC].T
    return out
```

```python
import numpy as np
from contextlib import ExitStack
import concourse.bass as bass
import concourse.mybir as mybir
from concourse.bass_utils import run_bass_kernel_spmd

F32 = mybir.dt.float32
BF16 = mybir.dt.bfloat16
AF = mybir.ActivationFunctionType
ALU = mybir.AluOpType

D = 2048
NCH = 16
DFF = 8192
DEPTH = 4
SEQ = 4096
BATCH = 4
TW = 352
TWS = [352, 344, 344]
NT = len(TWS)
TOFF = [0, 352, 696, 1040]
TB = TOFF[-1]
NBLK = 2
TC = TB * NBLK
FB = 1024
NFB = DFF // FB
KFB = FB // 128
RING = 4
TFW = TB + 32
RMS_EPS = 1e-6
LN_EPS = 1e-5
CONF_K = 31
N_CORES = 8
MAINB = [0, 1, 2, 3]
STATB = [4, 5, 6]
MODB = 7
NPE = 14


def _param_layout():
    off = {}
    n = 0
    for l in range(DEPTH):
        for name, w in (("modb", 96), ("pre_mix_g", 16), ("post_mix_g", 16), ("pre_ffn_g", 16), ("post_ffn_g", 16)):
            off[(l, name)] = n
            n += w
        if l % 2 == 0:
            for name, w in (("a_b1", 32), ("a_dw", CONF_K * 16), ("a_dwb", 16), ("a_ln_g", 16), ("a_ln_b", 16), ("a_b2", 16)):
                off[(l, name)] = n
                n += w
        else:
            off[(l, "b_conv")] = n
            n += 48
    return off, n


POFF, NPRM = _param_layout()


def _vec_cols(v):
    v = np.asarray(v, dtype=np.float32)
    return np.ascontiguousarray(v.reshape(-1, 128).T)


def _pack_params(inp):
    prm = np.zeros((128, NPRM), dtype=np.float32)

    def put(l, name, arr):
        a = _vec_cols(arr)
        prm[:, POFF[(l, name)]:POFF[(l, name)] + a.shape[1]] = a

    for l in range(DEPTH):
        put(l, "modb", inp["mod_b"][l])
        for nm in ("pre_mix_g", "post_mix_g", "pre_ffn_g", "post_ffn_g"):
            put(l, nm, inp[nm][l])
        j = l // 2
        if l % 2 == 0:
            put(l, "a_b1", inp["a_b1"][j])
            put(l, "a_dw", inp["a_dw"][j].reshape(-1))
            put(l, "a_dwb", inp["a_dwb"][j])
            put(l, "a_ln_g", inp["a_ln_g"][j])
            put(l, "a_ln_b", inp["a_ln_b"][j])
            put(l, "a_b2", inp["a_b2"][j])
        else:
            put(l, "b_conv", inp["b_conv"][j].reshape(-1))
    return prm


class Res:
    __slots__ = ("name", "w", "r")

    def __init__(self, name):
        self.name = name
        self.w = None
        self.r = {}


def _flat(x):
    out = []
    for e in x:
        if isinstance(e, (list, tuple)):
            out.extend(_flat(e))
        else:
            out.append(e)
    return out


class Trk:
    def __init__(self, nc, stack, dry=False):
        self.nc = nc
        self.dry = dry
        self.stack = stack
        self.eng = {"pe": nc.tensor, "act": nc.scalar, "dve": nc.vector, "pool": nc.gpsimd, "sp": nc.sync}
        self.sem = {}
        if not dry:
            for k in self.eng:
                self.sem[k] = stack.enter_context(nc.semaphore("s_" + k))
        self.cnt = {k: 0 for k in self.eng}
        self.waited = {k: {} for k in self.eng}
        self.nwaits = 0
        self.nops = 0
        self.self_sync = {"pe": False, "act": True, "dve": True, "pool": True, "sp": True}

    def dsem(self, name):
        if self.dry:
            return [None, 0]
        return [self.stack.enter_context(self.nc.semaphore(name)), 0]

    def _wait(self, e, sem, val):
        key = id(sem)
        if self.waited[e].get(key, 0) >= val:
            return
        self.eng[e].wait_ge(sem, val)
        self.waited[e][key] = val
        self.nwaits += 1

    def op(self, e, fn, reads=(), writes=(), dsem=None, ss=True):
        if self.dry:
            return None
        reads = _flat(reads)
        writes = _flat(writes)
        deps = []
        for r in reads:
            if r.w is not None:
                deps.append(r.w)
        for r in writes:
            if r.w is not None:
                deps.append(r.w)
            deps.extend(r.r.values())
        for (de, sem, val) in deps:
            if de == e and (not ss or not self.self_sync[e]):
                continue
            self._wait(e, sem, val)
        ins = fn()
        self.nops += 1
        if dsem is not None:
            for i_ in (ins if isinstance(ins, list) else [ins]):
                dsem[1] += 16
                i_.then_inc(dsem[0], 16)
            me = ("dma", dsem[0], dsem[1])
        else:
            self.cnt[e] += 1
            ins.then_inc(self.sem[e], 1)
            me = (e, self.sem[e], self.cnt[e])
        for r in reads:
            r.r[(me[0], id(me[1]))] = me
        for r in writes:
            r.w = me
            r.r = {}
        return me

    def wait_for(self, e, dep):
        if self.dry or dep is None:
            return
        self._wait(e, dep[1], dep[2])


def build_nc(layers=tuple(range(DEPTH)), first=True):
    nc = bass.Bass("TRN2", target_bir_lowering=False)
    xT = nc.dram_tensor("xT", [D, TC], F32, kind="ExternalInput").ap()
    cT = nc.dram_tensor("cT", [128, NCH], F32, kind="ExternalInput").ap()
    prm_d = nc.dram_tensor("prm", [128, NPRM], F32, kind="ExternalInput").ap()
    mod_w = nc.dram_tensor("mod_w", [DEPTH, D, 6 * D], F32, kind="ExternalInput").ap()
    a_w1 = nc.dram_tensor("a_w1", [2, D, 2 * D], F32, kind="ExternalInput").ap()
    a_w2 = nc.dram_tensor("a_w2", [2, D, D], F32, kind="ExternalInput").ap()
    b_w_in = nc.dram_tensor("b_w_in", [2, D, 3 * D], F32, kind="ExternalInput").ap()
    b_w_out = nc.dram_tensor("b_w_out", [2, D, D], F32, kind="ExternalInput").ap()
    f_w1 = nc.dram_tensor("f_w1", [DEPTH, D, DFF], F32, kind="ExternalInput").ap()
    f_w2 = nc.dram_tensor("f_w2", [DEPTH, DFF, D], F32, kind="ExternalInput").ap()
    ident_d = nc.dram_tensor("ident", [128, 128], F32, kind="ExternalInput").ap()
    outT = nc.dram_tensor("outT", [D, TC], F32, kind="ExternalOutput").ap()

    with ExitStack() as st_:
        sb = lambda name, shape, dt: st_.enter_context(nc.sbuf_tensor(name, shape, dt))
        ring = sb("ring", [128, RING, 16, 2, 128], BF16)
        hb = sb("hb", [128, NCH, TB], BF16)
        ub = sb("ub", [128, KFB, TB], BF16)
        BIG = sb("BIG", [128, NCH, TB], F32)
        tf = sb("tf", [128, 4, TFW], F32)
        sqb = sb("sqb", [128, 4, TB], BF16)
        stt = sb("stt", [128, 6, TW], F32)
        v16 = sb("v16", [128, 3, TFW], BF16)
        dg = sb("dg", [128, 2, NPE, 128], BF16)
        ident = sb("ident_s", [128, 128], F32)
        hv = sb("hv", [128, NCH, 32], F32)
        hp = sb("hp", [128, NCH, 2], F32)
        prm = sb("prm_s", [128, NPRM], F32)
        modT = sb("modT", [128, 2, 96], F32)
        der = sb("der", [128, 2, 64], F32)
        cab = sb("cab", [128, NCH], BF16)
        cTf = sb("cTf", [128, NCH], F32)
        ones = sb("ones", [128, 128], BF16)
        epsr = sb("epsr", [128, 2], F32)
        ps = st_.enter_context(nc.psum_tensor("ps", [128, 8, 512], F32))

        def emit(T, W):
            R_ring = [Res("ring%d" % i) for i in range(RING)]
            S_ring = [T.dsem("dring%d" % i) for i in range(RING)]
            R_hb = [[Res("hb") for t in range(NT)] for c in range(NCH)]
            R_ub = [[Res("ub") for t in range(NT)] for k in range(KFB)]
            R_big = [[Res("big") for t in range(NT)] for c in range(NCH)]
            S_big = [T.dsem("dbig%d" % c) for c in range(NCH)]
            R_tf = [[Res("tf%d" % i)] + ([Res("tft") for t in range(NT)] if i >= 2 else []) for i in range(4)]
            S_tf = [T.dsem("dtf%d" % i) for i in range(4)]
            R_sqb = [Res("sqb%d" % i) for i in range(4)]
            R_st = [Res("st%d" % i) for i in range(6)]
            R_v16 = [Res("v16_0"), Res("v16_1"), Res("v16_2")]
            R_dg = [Res("dg0"), Res("dg1")]
            R_ident = Res("ident")
            S_ident = T.dsem("dident")
            R_bank = [Res("bank%d" % i) for i in range(8)]
            R_hv = [Res("hv") for c in range(NCH)]
            R_hp = [Res("hp") for c in range(NCH)]
            R_prm = Res("prm")
            S_prm = T.dsem("dprm")
            R_cT = Res("cT")
            S_cT = T.dsem("dcT")
            R_cab = Res("cab")
            R_ones = Res("ones")
            R_modT = [Res("modT0"), Res("modT1")]
            R_der = [Res("der0"), Res("der1")]
            R_x = [[Res("x") for b in range(NBLK)] for c in range(NCH)]
            S_x = [[T.dsem("dx%d_%d" % (c, b)) for b in range(NBLK)] for c in range(NCH)]
            state = {"bank": 0, "sq": 0, "st": 0}

            def nbank():
                b = MAINB[state["bank"] % len(MAINB)]
                state["bank"] += 1
                return b

            def nsq():
                i = state["sq"] % 4
                state["sq"] += 1
                return i

            def nst():
                i = state["st"] % 6
                state["st"] += 1
                return i

            def P(l, name, lo, n=1):
                o = POFF[(l, name)] + lo
                return prm[:, o:o + n]

            tok = lambda t: slice(TOFF[t], TOFF[t + 1])
            rows = lambda c: slice(c * 128, (c + 1) * 128)
            cols = lambda b: slice(b * TB, (b + 1) * TB)

            def lhs16(s, c, jj):
                return ring[:, s, c, jj, :]

            def lhs8(s, k, jj):
                return ring[:, s, 2 * k + jj // 2, jj % 2, :]

            def issue_unit(desc, s):
                kind = desc[0]
                dst = ring[:, s]
                if kind == "mod":
                    _, l, u = desc
                    src = mod_w[l, :, u * 256:(u + 1) * 256].rearrange("(c p) (h n) -> p c h n", p=128, h=2)
                elif kind == "a1":
                    _, la, j = desc
                    src = [a_w1[la, :, h * D + j * 128:h * D + (j + 1) * 128].rearrange("(c p) n -> p c n", p=128) for h in range(2)]
                elif kind == "a2":
                    _, la, eu = desc
                    src = a_w2[la, :, eu * 256:(eu + 1) * 256].rearrange("(c p) (h n) -> p c h n", p=128, h=2)
                elif kind == "bcx":
                    _, lb, j = desc
                    src = [b_w_in[lb, :, (h + 1) * D + j * 128:(h + 1) * D + (j + 1) * 128].rearrange("(c p) n -> p c n", p=128) for h in range(2)]
                elif kind == "bb":
                    _, lb, eu = desc
                    src = b_w_in[lb, :, eu * 256:(eu + 1) * 256].rearrange("(c p) (h n) -> p c h n", p=128, h=2)
                elif kind == "bo":
                    _, lb, eu = desc
                    src = b_w_out[lb, :, eu * 256:(eu + 1) * 256].rearrange("(c p) (h n) -> p c h n", p=128, h=2)
                elif kind == "f1":
                    _, l, fb, u = desc
                    c0 = fb * FB + u * 256
                    src = f_w1[l, :, c0:c0 + 256].rearrange("(c p) (h n) -> p c h n", p=128, h=2)
                elif kind == "f2":
                    _, l, fb, u = desc
                    src = f_w2[l, fb * FB:(fb + 1) * FB, u * 512:(u + 1) * 512].rearrange(
                        "(k p) n -> p k n", p=128)
                else:
                    raise ValueError(kind)
                if isinstance(src, list):
                    T.op("pool", lambda: [nc.gpsimd.dma_start(out=ring[:, s, :, h, :], in_=src[h]) for h in range(2)],
                         writes=[R_ring[s]], dsem=S_ring[s])
                else:
                    T.op("pool", lambda: nc.gpsimd.dma_start(out=dst, in_=src), writes=[R_ring[s]], dsem=S_ring[s])

            W.issue = issue_unit

            def group(out_ap, lhs_fn, rhs_fn, n):
                ins = None
                for c in range(n):
                    ins = nc.tensor.matmul(out_ap, lhsT=lhs_fn(c), rhs=rhs_fn(c), start=(c == 0), stop=(c == n - 1))
                return ins

            def setup():
                T.op("sp", lambda: nc.sync.dma_start(out=prm[:], in_=prm_d), writes=[R_prm], dsem=S_prm)
                T.op("sp", lambda: nc.sync.dma_start(out=cTf[:], in_=cT), writes=[R_cT], dsem=S_cT)
                T.op("sp", lambda: nc.sync.dma_start(out=ident[:], in_=ident_d), writes=[R_ident], dsem=S_ident)
                T.op("dve", lambda: nc.vector.memset(ones[:], 1.0), writes=[R_ones])
                T.op("dve", lambda: nc.vector.memset(epsr[:, 0:1], RMS_EPS), writes=[R_ones])
                T.op("dve", lambda: nc.vector.memset(epsr[:, 1:2], LN_EPS), writes=[R_ones])
                T.op("act", lambda: nc.scalar.activation(out=cab[:], in_=cTf[:], func=AF.Silu), reads=[R_cT], writes=[R_cab])

            def mod_unit(l, u):
                s = W.get(("mod", l, u))

                def f():
                    ins = None
                    for jj in range(2):
                        col = u * 2 + jj
                        ins = group(ps[:, MODB, col:col + 1], lambda c: lhs16(s, c, jj), lambda c: cab[:, c:c + 1], NCH)
                    return ins
                T.op("pe", f, reads=[R_ring[s], R_cab], writes=[R_bank[MODB]])

            MOD_PIECES = [(0, 32, 16, (0, 16, "pre_mix_g", True)), (32, 48, 24, (1, 32, "post_mix_g", False)),
                          (48, 80, 40, (2, 64, "pre_ffn_g", True)), (80, 96, 48, (3, 80, "post_ffn_g", False))]

            def mod_piece(l, pi):
                p = l % 2
                c0, c1, _, (i, sc0, gname, is_gs) = MOD_PIECES[pi]
                T.op("dve", lambda: nc.vector.tensor_tensor(out=modT[:, p, c0:c1], in0=ps[:, MODB, c0:c1], in1=P(l, "modb", c0, c1 - c0), op=ALU.add),
                     reads=[R_bank[MODB], R_prm], writes=[R_modT[p]])
                o = der[:, p, i * 16:(i + 1) * 16]
                if is_gs:
                    fn = lambda: nc.vector.scalar_tensor_tensor(out=o, in0=modT[:, p, sc0:sc0 + 16], scalar=1.0, in1=P(l, gname, 0, 16),
                                                                op0=ALU.add, op1=ALU.mult)
                else:
                    fn = lambda: nc.vector.tensor_tensor(out=o, in0=modT[:, p, sc0:sc0 + 16], in1=P(l, gname, 0, 16), op=ALU.mult)
                T.op("dve", fn, reads=[R_modT[p], R_prm], writes=[R_der[p]])

            def mod_finish(l):
                for pi in range(4):
                    mod_piece(l, pi)

            lazy = {"l": None, "next_u": 48, "next_piece": 4}

            def mod_tick():
                if lazy["l"] is None or lazy["next_u"] >= 48:
                    return
                mod_unit(lazy["l"], lazy["next_u"])
                lazy["next_u"] += 1
                while lazy["next_piece"] < 4 and MOD_PIECES[lazy["next_piece"]][2] <= lazy["next_u"]:
                    mod_piece(lazy["l"], lazy["next_piece"])
                    lazy["next_piece"] += 1

            def mod_need(pi):
                while lazy["l"] is not None and lazy["next_piece"] <= pi:
                    mod_tick()

            S_fb = [T.dsem("dfb%d" % i) for i in range(14)]

            def sbuf_f32(i):
                if i < 8:
                    ap = hb[:, 2 * i:2 * i + 2, :].rearrange("p a b -> p (a b)").bitcast(F32)
                    res = [R_hb[2 * i], R_hb[2 * i + 1]]
                elif i < 12:
                    j = i - 8
                    ap = ub[:, 2 * j:2 * j + 2, :].rearrange("p a b -> p (a b)").bitcast(F32)
                    res = R_ub[2 * j] + R_ub[2 * j + 1]
                else:
                    ap = tf[:, i - 12, 0:TB]
                    res = [R_tf[i - 12]]
                return ap, res, S_fb[i]

            def stat_chunk(c):
                i = nsq()
                T.op("act", lambda: nc.scalar.activation(out=sqb[:, i, :], in_=BIG[:, c, :], func=AF.Square),
                     reads=R_big[c], writes=[R_sqb[i]])

                def f():
                    ins = None
                    for t in range(NT):
                        ins = nc.tensor.matmul(ps[:, STATB[t], 0:TWS[t]], lhsT=ones[:], rhs=sqb[:, i, tok(t)],
                                               start=(c == 0), stop=(c == NCH - 1))
                    return ins
                T.op("pe", f, reads=[R_sqb[i], R_ones], writes=[R_bank[b] for b in STATB])

            STAT_LAG = 2
            pend = []

            def stat_lag(c):
                pend.append(c)
                if len(pend) > STAT_LAG:
                    stat_chunk(pend.pop(0))

            def stat_flush():
                while pend:
                    stat_chunk(pend.pop(0))

            def rstd_finish():
                stat_flush()
                for t in range(NT):
                    T.op("act", lambda: nc.scalar.activation(out=tf[:, 3, tok(t)], in_=ps[:, STATB[t], 0:TWS[t]], func=AF.Sqrt,
                                                             scale=1.0 / D, bias=epsr[:, 0:1]),
                         reads=[R_bank[STATB[t]], R_ones], writes=[R_tf[3][1 + t]])
                for t in range(NT):
                    T.op("dve", lambda: nc.vector.reciprocal(out=tf[:, 2, tok(t)], in_=tf[:, 3, tok(t)]),
                         reads=[R_tf[3][1 + t]], writes=[R_tf[2][1 + t]])

            def prenorm(src, blk, gs, sh, rd_gs, rd_sh, mode):
                if mode == "load":
                    for c in range(NCH):
                        T.op("sp", lambda: nc.sync.dma_start(out=BIG[:, c, :], in_=src[rows(c), cols(blk)]),
                             reads=[R_x[c][blk]], writes=R_big[c], dsem=S_big[c])
                if mode != "stats":
                    for c in range(NCH):
                        stat_chunk(c)
                rstd_finish()
                for t in range(NT):
                    for c in range(NCH):
                        r = c % 2
                        T.op("dve", lambda: nc.vector.tensor_tensor(out=tf[:, r, tok(t)], in0=BIG[:, c, tok(t)], in1=tf[:, 2, tok(t)], op=ALU.mult),
                             reads=[R_big[c][t], R_tf[2][1 + t]], writes=[R_tf[r]])
                        T.op("act", lambda: nc.scalar.activation(out=hb[:, c, tok(t)], in_=tf[:, r, tok(t)], func=AF.Identity,
                                                                 scale=gs[:, c:c + 1], bias=sh[:, c:c + 1]),
                             reads=[R_tf[r], rd_gs, rd_sh], writes=[R_hb[c][t]])

            def postnorm_residual(src, blk, gp, rd_gp, keep, next_load=None):
                mod_need(1 if keep else 3)
                rstd_finish()
                if keep:
                    kb = [sbuf_f32(8 + i) for i in range(4)] + [sbuf_f32(12), sbuf_f32(13)]
                    nb = len(kb)

                    def xload(c):
                        ap, res, ds = kb[c % nb]
                        T.op("sp", lambda: nc.sync.dma_start(out=ap, in_=src[rows(c), cols(blk)]),
                             reads=[R_x[c][blk]], writes=res, dsem=ds)
                    for c in range(nb):
                        xload(c)
                    for c in range(NCH):
                        ap, res, ds = kb[c % nb]
                        T.op("dve", lambda: nc.vector.scalar_tensor_tensor(out=BIG[:, c, :], in0=BIG[:, c, :], scalar=gp[:, c:c + 1],
                                                                          in1=tf[:, 2, 0:TB], op0=ALU.mult, op1=ALU.mult),
                             reads=R_big[c] + [R_tf[2], rd_gp], writes=R_big[c])
                        en, E = ("pool", nc.gpsimd) if c % 3 == 1 else ("dve", nc.vector)
                        T.op(en, lambda: E.tensor_tensor(out=BIG[:, c, :], in0=BIG[:, c, :], in1=ap, op=ALU.add),
                             reads=R_big[c] + res, writes=R_big[c])
                        T.op("sp", lambda: nc.sync.dma_start(out=outT[rows(c), cols(blk)], in_=BIG[:, c, :]),
                             reads=R_big[c], writes=[R_x[c][blk]], dsem=S_x[c][blk])
                        if c + nb < NCH:
                            xload(c + nb)
                        stat_chunk(c)
                else:
                    for c in range(NCH):
                        ap, res, _ = sbuf_f32(c % 14)
                        T.op("dve", lambda: nc.vector.scalar_tensor_tensor(out=ap, in0=BIG[:, c, :], scalar=gp[:, c:c + 1],
                                                                          in1=tf[:, 2, 0:TB], op0=ALU.mult, op1=ALU.mult),
                             reads=R_big[c] + [R_tf[2], rd_gp], writes=res)
                        T.op("pool", lambda: nc.gpsimd.dma_start(out=outT[rows(c), cols(blk)], in_=ap, accum_op=ALU.add),
                             reads=res, writes=[R_x[c][blk]], dsem=S_x[c][blk])
                        if next_load is not None:
                            next_load(c)

            def make_next_load(nsrc, nblk):
                def nl(c):
                    T.op("sp", lambda: nc.sync.dma_start(out=BIG[:, c, :], in_=nsrc[rows(c), cols(nblk)]),
                         reads=[R_x[c][nblk]], writes=R_big[c], dsem=S_big[c])
                return nl

            def mixer_a(l, blk, src, preloaded):
                la = l // 2
                p = l % 2
                prenorm(src, blk, der[:, p, 0:16], modT[:, p, 0:16], R_der[p], R_modT[p], "loaded" if preloaded else "load")
                def a_glu(j):
                    vq = j % 3
                    dq = j % 2
                    for k in range(NPE):
                        T.op("act", lambda: nc.scalar.activation(out=dg[:, dq, k, :], in_=ident[:], func=AF.Identity,
                                                                 scale=P(l, "a_dw", k * 16 + j)),
                             reads=[R_ident, R_prm], writes=[R_dg[dq]], ss=(k == 0))
                    mod_tick()
                    s = W.get(("a1", la, j))
                    if blk == 0:
                        T.op("dve", lambda: nc.vector.memset(v16[:, vq, 0:32], 0.0), writes=[R_v16[vq]])
                    else:
                        T.op("act", lambda: nc.scalar.copy(out=v16[:, vq, 2:32], in_=hv[:, j, 0:30]),
                             reads=[R_hv[j]], writes=[R_v16[vq]])
                    for t in range(NT):
                        ba = nbank()
                        T.op("pe", lambda: group(ps[:, ba, 0:TWS[t]], lambda c: ring[:, s, c, 0, :], lambda c: hb[:, c, tok(t)], NCH),
                             reads=[R_ring[s]] + [R_hb[c_][t] for c_ in range(NCH)], writes=[R_bank[ba]])
                        bg = nbank()
                        T.op("pe", lambda: group(ps[:, bg, 0:TWS[t]], lambda c: ring[:, s, c, 1, :], lambda c: hb[:, c, tok(t)], NCH),
                             reads=[R_ring[s]] + [R_hb[c_][t] for c_ in range(NCH)], writes=[R_bank[bg]])
                        ka = nst()
                        T.op("act", lambda: nc.scalar.activation(out=stt[:, ka, 0:TWS[t]], in_=ps[:, ba, 0:TWS[t]], func=AF.Identity,
                                                                 bias=P(l, "a_b1", j), scale=1.0),
                             reads=[R_bank[ba], R_prm], writes=[R_st[ka]])
                        ks = nst()
                        T.op("act", lambda: nc.scalar.activation(out=stt[:, ks, 0:TWS[t]], in_=ps[:, bg, 0:TWS[t]], func=AF.Sigmoid,
                                                                 bias=P(l, "a_b1", 16 + j), scale=1.0),
                             reads=[R_bank[bg], R_prm], writes=[R_st[ks]])
                        T.op("dve", lambda: nc.vector.tensor_tensor(out=v16[:, vq, 32 + TOFF[t]:32 + TOFF[t + 1]], in0=stt[:, ka, 0:TWS[t]],
                                                                   in1=stt[:, ks, 0:TWS[t]], op=ALU.mult),
                             reads=[R_st[ka], R_st[ks]], writes=[R_v16[vq]])
                    if blk < NBLK - 1:
                        T.op("act", lambda: nc.scalar.copy(out=hv[:, j, 0:30], in_=v16[:, vq, 32 + TB - 30:32 + TB]),
                             reads=[R_v16[vq]], writes=[R_hv[j]])

                def a_conv(j):
                    vq = j % 3
                    dq = j % 2
                    for t in range(NT):
                        b = nbank()
                        T.op("pe", lambda: group(ps[:, b, 0:TWS[t]], lambda k: dg[:, dq, k, :],
                                                 lambda k: v16[:, vq, 2 + k + TOFF[t]:2 + k + TOFF[t + 1]], NPE),
                             reads=[R_dg[dq], R_v16[vq]], writes=[R_bank[b]])
                        T.op("act", lambda: nc.scalar.activation(out=BIG[:, j, tok(t)], in_=ps[:, b, 0:TWS[t]], func=AF.Identity,
                                                                 bias=P(l, "a_dwb", j), scale=1.0),
                             reads=[R_bank[b], R_prm], writes=[R_big[j][t]])
                    for k in range(NPE, CONF_K):
                        T.op("dve", lambda: nc.vector.scalar_tensor_tensor(out=BIG[:, j, :], in0=v16[:, vq, 2 + k:2 + k + TB],
                                                                          scalar=P(l, "a_dw", k * 16 + j), in1=BIG[:, j, :],
                                                                          op0=ALU.mult, op1=ALU.add),
                             reads=[R_v16[vq], R_prm] + R_big[j], writes=R_big[j], ss=(k == NPE))

                for j in range(NCH + 1):
                    if j < NCH:
                        a_glu(j)
                    if j >= 1:
                        a_conv(j - 1)
                QB = [1, 2, 3]
                for j in range(NCH):
                    i0 = nsq()
                    i1 = nsq()
                    T.op("act", lambda: nc.scalar.copy(out=sqb[:, i0, :], in_=BIG[:, j, :]), reads=R_big[j], writes=[R_sqb[i0]])
                    T.op("act", lambda: nc.scalar.activation(out=sqb[:, i1, :], in_=BIG[:, j, :], func=AF.Square),
                         reads=R_big[j], writes=[R_sqb[i1]])

                    def f():
                        ins = None
                        for t in range(NT):
                            nc.tensor.matmul(ps[:, STATB[t], 0:TWS[t]], lhsT=ones[:], rhs=sqb[:, i0, tok(t)], start=(j == 0), stop=(j == NCH - 1))
                            ins = nc.tensor.matmul(ps[:, QB[t], 0:TWS[t]], lhsT=ones[:], rhs=sqb[:, i1, tok(t)], start=(j == 0), stop=(j == NCH - 1))
                        return ins
                    T.op("pe", f, reads=[R_sqb[i0], R_sqb[i1], R_ones], writes=[R_bank[b] for b in STATB + QB])
                for t in range(NT):
                    tk = tok(t)
                    T.op("dve", lambda: nc.vector.tensor_scalar(out=tf[:, 3, tk], in0=ps[:, STATB[t], 0:TWS[t]], scalar1=1.0 / D, scalar2=None, op0=ALU.mult),
                         reads=[R_bank[STATB[t]]], writes=[R_tf[3][1 + t]])
                    T.op("dve", lambda: nc.vector.tensor_tensor(out=tf[:, 0, tk], in0=tf[:, 3, tk], in1=tf[:, 3, tk], op=ALU.mult),
                         reads=[R_tf[3][1 + t]], writes=[R_tf[0]])
                    T.op("dve", lambda: nc.vector.scalar_tensor_tensor(out=tf[:, 0, tk], in0=ps[:, QB[t], 0:TWS[t]], scalar=1.0 / D,
                                                                      in1=tf[:, 0, tk], op0=ALU.mult, op1=ALU.subtract),
                         reads=[R_bank[QB[t]], R_tf[0]], writes=[R_tf[0]])
                    T.op("act", lambda: nc.scalar.activation(out=tf[:, 0, tk], in_=tf[:, 0, tk], func=AF.Sqrt, scale=1.0, bias=epsr[:, 1:2]),
                         reads=[R_tf[0], R_ones], writes=[R_tf[0]])
                    T.op("dve", lambda: nc.vector.reciprocal(out=tf[:, 2, tk], in_=tf[:, 0, tk]), reads=[R_tf[0]], writes=[R_tf[2][1 + t]])
                    T.op("dve", lambda: nc.vector.scalar_tensor_tensor(out=tf[:, 3, tk], in0=tf[:, 3, tk], scalar=-1.0, in1=tf[:, 2, tk],
                                                                      op0=ALU.mult, op1=ALU.mult),
                         reads=[R_tf[3][1 + t], R_tf[2][1 + t]], writes=[R_tf[3][1 + t]])
                for t in range(NT):
                    tk = tok(t)
                    for j in range(NCH):
                        i = nst()
                        T.op("dve", lambda: nc.vector.tensor_tensor(out=stt[:, i, 0:TWS[t]], in0=BIG[:, j, tk], in1=tf[:, 2, tk], op=ALU.mult),
                             reads=[R_big[j][t], R_tf[2][1 + t]], writes=[R_st[i]])
                        T.op("dve", lambda: nc.vector.tensor_tensor(out=stt[:, i, 0:TWS[t]], in0=stt[:, i, 0:TWS[t]], in1=tf[:, 3, tk], op=ALU.add),
                             reads=[R_st[i], R_tf[3][1 + t]], writes=[R_st[i]])
                        T.op("act", lambda: nc.scalar.activation(out=hb[:, j, tk], in_=stt[:, i, 0:TWS[t]], func=AF.Silu,
                                                                 scale=P(l, "a_ln_g", j), bias=P(l, "a_ln_b", j)),
                             reads=[R_st[i], R_prm], writes=[R_hb[j][t]])
                for eu in range(8):
                    mod_tick()
                    s = W.get(("a2", la, eu))
                    for jj in range(2):
                        e = eu * 2 + jj
                        for t in range(NT):
                            b = nbank()
                            T.op("pe", lambda: group(ps[:, b, 0:TWS[t]], lambda c: lhs16(s, c, jj), lambda c: hb[:, c, tok(t)], NCH),
                                 reads=[R_ring[s]] + [R_hb[c_][t] for c_ in range(NCH)], writes=[R_bank[b]])
                            T.op("act", lambda: nc.scalar.activation(out=BIG[:, e, tok(t)], in_=ps[:, b, 0:TWS[t]], func=AF.Identity,
                                                                     bias=P(l, "a_b2", e), scale=1.0),
                                 reads=[R_bank[b], R_prm], writes=[R_big[e][t]])
                        stat_lag(e)
                postnorm_residual(src, blk, der[:, p, 16:32], R_der[p], keep=True)

            def mixer_b(l, blk, src, preloaded):
                lb = l // 2
                p = l % 2
                prenorm(src, blk, der[:, p, 0:16], modT[:, p, 0:16], R_der[p], R_modT[p], "loaded" if preloaded else "load")
                for j in range(NCH):
                    mod_tick()
                    s = W.get(("bcx", lb, j))
                    vr = j % 2
                    if blk == 0:
                        T.op("dve", lambda: nc.vector.memset(tf[:, vr, 0:32], 0.0), writes=[R_tf[vr]])
                    else:
                        T.op("dve", lambda: nc.vector.tensor_copy(out=tf[:, vr, 30:32], in_=hp[:, j, 0:2]),
                             reads=[R_hp[j]], writes=[R_tf[vr]])
                    for t in range(NT):
                        bc = nbank()
                        T.op("pe", lambda: group(ps[:, bc, 0:TWS[t]], lambda c: ring[:, s, c, 0, :], lambda c: hb[:, c, tok(t)], NCH),
                             reads=[R_ring[s]] + [R_hb[c_][t] for c_ in range(NCH)], writes=[R_bank[bc]])
                        bx = nbank()
                        T.op("pe", lambda: group(ps[:, bx, 0:TWS[t]], lambda c: ring[:, s, c, 1, :], lambda c: hb[:, c, tok(t)], NCH),
                             reads=[R_ring[s]] + [R_hb[c_][t] for c_ in range(NCH)], writes=[R_bank[bx]])
                        k = nst()
                        T.op("act", lambda: nc.scalar.copy(out=stt[:, k, 0:TWS[t]], in_=ps[:, bc, 0:TWS[t]]), reads=[R_bank[bc]], writes=[R_st[k]])
                        T.op("dve", lambda: nc.vector.tensor_tensor(out=tf[:, vr, 32 + TOFF[t]:32 + TOFF[t + 1]], in0=ps[:, bx, 0:TWS[t]],
                                                                   in1=stt[:, k, 0:TWS[t]], op=ALU.mult),
                             reads=[R_bank[bx], R_st[k]], writes=[R_tf[vr]])
                    if blk < NBLK - 1:
                        T.op("act", lambda: nc.scalar.copy(out=hp[:, j, 0:2], in_=tf[:, vr, 32 + TB - 2:32 + TB]),
                             reads=[R_tf[vr]], writes=[R_hp[j]])
                    T.op("act", lambda: nc.scalar.activation(out=BIG[:, j, :], in_=tf[:, vr, 30:30 + TB], func=AF.Identity,
                                                             scale=P(l, "b_conv", 0 * 16 + j)),
                         reads=[R_tf[vr], R_prm], writes=R_big[j])
                    for k in range(1, 3):
                        T.op("dve", lambda: nc.vector.scalar_tensor_tensor(out=BIG[:, j, :], in0=tf[:, vr, 30 + k:30 + k + TB],
                                                                          scalar=P(l, "b_conv", k * 16 + j), in1=BIG[:, j, :],
                                                                          op0=ALU.mult, op1=ALU.add),
                             reads=[R_tf[vr], R_prm] + R_big[j], writes=R_big[j])
                for eu in range(8):
                    mod_tick()
                    s = W.get(("bb", lb, eu))
                    for jj in range(2):
                        e = eu * 2 + jj
                        for t in range(NT):
                            b = nbank()
                            T.op("pe", lambda: group(ps[:, b, 0:TWS[t]], lambda c: lhs16(s, c, jj), lambda c: hb[:, c, tok(t)], NCH),
                                 reads=[R_ring[s]] + [R_hb[c_][t] for c_ in range(NCH)], writes=[R_bank[b]])
                            T.op("dve", lambda: nc.vector.tensor_tensor(out=BIG[:, e, tok(t)], in0=ps[:, b, 0:TWS[t]], in1=BIG[:, e, tok(t)], op=ALU.mult),
                                 reads=[R_bank[b], R_big[e][t]], writes=[R_big[e][t]])
                for c in range(NCH):
                    T.op("act", lambda: nc.scalar.copy(out=hb[:, c, :], in_=BIG[:, c, :]), reads=R_big[c], writes=[R_hb[c]])
                for eu in range(8):
                    mod_tick()
                    s = W.get(("bo", lb, eu))
                    for jj in range(2):
                        e = eu * 2 + jj
                        for t in range(NT):
                            b = nbank()
                            T.op("pe", lambda: group(ps[:, b, 0:TWS[t]], lambda c: lhs16(s, c, jj), lambda c: hb[:, c, tok(t)], NCH),
                                 reads=[R_ring[s]] + [R_hb[c_][t] for c_ in range(NCH)], writes=[R_bank[b]])
                            T.op("act", lambda: nc.scalar.copy(out=BIG[:, e, tok(t)], in_=ps[:, b, 0:TWS[t]]),
                                 reads=[R_bank[b]], writes=[R_big[e][t]])
                        stat_lag(e)
                postnorm_residual(src, blk, der[:, p, 16:32], R_der[p], keep=True)

            def ffn(l, blk, next_mod, next_load):
                p = l % 2
                mod_need(2)
                prenorm(outT, blk, der[:, p, 32:48], modT[:, p, 48:64], R_der[p], R_modT[p], "stats")
                for fb in range(NFB):
                    for u in range(4):
                        mod_tick()
                        s = W.get(("f1", l, fb, u))
                        for jj in range(2):
                            kk = u * 2 + jj
                            for t in range(NT):
                                b = nbank()
                                T.op("pe", lambda: group(ps[:, b, 0:TWS[t]], lambda c: lhs16(s, c, jj), lambda c: hb[:, c, tok(t)], NCH),
                                     reads=[R_ring[s]] + [R_hb[c_][t] for c_ in range(NCH)], writes=[R_bank[b]])
                                i = nst()
                                T.op("act", lambda: nc.scalar.activation(out=stt[:, i, 0:TWS[t]], in_=ps[:, b, 0:TWS[t]], func=AF.Relu),
                                     reads=[R_bank[b]], writes=[R_st[i]])
                                T.op("dve", lambda: nc.vector.tensor_tensor(out=ub[:, kk, tok(t)], in0=stt[:, i, 0:TWS[t]], in1=stt[:, i, 0:TWS[t]], op=ALU.mult),
                                     reads=[R_st[i]], writes=[R_ub[kk][t]])
                        if next_mod is not None and u < 3:
                            mod_unit(next_mod, (blk * NFB + fb) * 3 + u)
                    for u in range(4):
                        mod_tick()
                        s = W.get(("f2", l, fb, u))
                        for jj in range(4):
                            e = u * 4 + jj
                            for t in range(NT):
                                b = nbank()
                                T.op("pe", lambda: group(ps[:, b, 0:TWS[t]], lambda k: lhs8(s, k, jj), lambda k: ub[:, k, tok(t)], KFB),
                                     reads=[R_ring[s]] + [R_ub[k][t] for k in range(KFB)], writes=[R_bank[b]])
                                if fb == 0:
                                    T.op("act", lambda: nc.scalar.copy(out=BIG[:, e, tok(t)], in_=ps[:, b, 0:TWS[t]]),
                                         reads=[R_bank[b]], writes=[R_big[e][t]])
                                else:
                                    T.op("dve", lambda: nc.vector.tensor_tensor(out=BIG[:, e, tok(t)], in0=ps[:, b, 0:TWS[t]], in1=BIG[:, e, tok(t)], op=ALU.add),
                                         reads=[R_bank[b], R_big[e][t]], writes=[R_big[e][t]])
                            if fb == NFB - 1:
                                stat_lag(e)
                postnorm_residual(outT, blk, der[:, p, 48:64], R_der[p], keep=False, next_load=next_load)

            setup()
            for li, l in enumerate(layers):
                if li == 0:
                    for u in range(16):
                        mod_unit(l, u)
                    mod_piece(l, 0)
                    lazy.update(l=l, next_u=16, next_piece=1)
                else:
                    mod_need(3)
                    mod_finish(l)
                nxt = layers[li + 1] if li + 1 < len(layers) else None
                for blk in range(NBLK):
                    src = xT if (first and li == 0) else outT
                    pre = state.get("preloaded", False)
                    if l % 2 == 0:
                        mixer_a(l, blk, src, pre)
                    else:
                        mixer_b(l, blk, src, pre)
                    if blk + 1 < NBLK:
                        nl = make_next_load(src, blk + 1)
                    elif li + 1 < len(layers):
                        nl = make_next_load(outT, 0)
                    else:
                        nl = None
                    state["preloaded"] = nl is not None
                    ffn(l, blk, nxt, nl)
            for c in range(NCH):
                for b in range(NBLK):
                    T.wait_for("sp", R_x[c][b].w)

        class WStream:
            def __init__(self, plan):
                self.plan = plan
                self.rec = []
                self.i = 0
                self.issued = 0
                self.issue = None

            def get(self, desc):
                if self.plan is None:
                    self.rec.append(desc)
                    return 0
                assert self.plan[self.i] == desc, (self.plan[self.i], desc)
                while self.issued < min(len(self.plan), self.i + RING):
                    self.issue(self.plan[self.issued], self.issued % RING)
                    self.issued += 1
                s = self.i % RING
                self.i += 1
                return s

        dryW = WStream(None)
        emit(Trk(nc, st_, dry=True), dryW)
        T = Trk(nc, st_)
        W = WStream(dryW.rec)
        emit(T, W)
        assert W.i == len(W.plan)
        build_nc.stats = dict(nops=T.nops, nwaits=T.nwaits, cnt=dict(T.cnt), units=len(W.plan))
    return nc


_WKEYS = ("mod_w", "a_w1", "a_w2", "b_w_in", "b_w_out", "f_w1", "f_w2")
LAYERS_PER_LAUNCH = DEPTH


def kernel(**inputs):
    inp = {k: np.asarray(v) for k, v in inputs.items()}
    x = inp["x"].astype(np.float32, copy=False)
    c = inp["c"].astype(np.float32, copy=False)
    prm = _pack_params(inp)
    wts = {k: np.ascontiguousarray(inp[k], dtype=np.float32) for k in _WKEYS}
    starts = [0, SEQ - TC]
    cur = []
    for k in range(N_CORES):
        b, h = k // 2, k % 2
        cur.append(np.ascontiguousarray(x[b, starts[h]:starts[h] + TC, :].T))
    for l0 in range(0, DEPTH, LAYERS_PER_LAUNCH):
        layers = tuple(range(l0, min(DEPTH, l0 + LAYERS_PER_LAUNCH)))
        nc = build_nc(layers=layers, first=True)
        in_maps = []
        for k in range(N_CORES):
            b = k // 2
            m = {"xT": cur[k], "cT": _vec_cols(c[b]), "prm": prm, "ident": np.eye(128, dtype=np.float32)}
            m.update(wts)
            in_maps.append(m)
        res = run_bass_kernel_spmd(nc, in_maps, core_ids=list(range(N_CORES)))
        cur = [np.asarray(res.results[k]["outT"]) for k in range(N_CORES)]
    out = np.empty((BATCH, SEQ, D), dtype=np.float32)
    for k in range(N_CORES):
        b, h = k // 2, k % 2
        if h == 0:
            out[b, 0:TC, :] = cur[k].T
        else:
            out[b, TC:SEQ, :] = cur[k][:, 2 * TC - SEQ:TC].T
    return out
```

```python
import numpy as np
from contextlib import ExitStack
import concourse.bass as bass
import concourse.mybir as mybir
from concourse.bass_utils import run_bass_kernel_spmd

F32 = mybir.dt.float32
BF16 = mybir.dt.bfloat16
AF = mybir.ActivationFunctionType
ALU = mybir.AluOpType

D = 2048
NCH = 16
DFF = 8192
DEPTH = 4
SEQ = 4096
BATCH = 4
TW = 352
TWS = [352, 344, 344]
NT = len(TWS)
TOFF = [0, 352, 696, 1040]
TB = TOFF[-1]
NBLK = 2
TC = TB * NBLK
FB = 1024
NFB = DFF // FB
KFB = FB // 128
RING = 4
TFW = TB + 32
RMS_EPS = 1e-6
LN_EPS = 1e-5
CONF_K = 31
N_CORES = 8
MAINB = [0, 1, 2, 3]
STATB = [4, 5, 6]
MODB = 7
NPE = 14


def _param_layout():
    off = {}
    n = 0
    for l in range(DEPTH):
        for name, w in (("modb", 96), ("pre_mix_g", 16), ("post_mix_g", 16), ("pre_ffn_g", 16), ("post_ffn_g", 16)):
            off[(l, name)] = n
            n += w
        if l % 2 == 0:
            for name, w in (("a_b1", 32), ("a_dw", CONF_K * 16), ("a_dwb", 16), ("a_ln_g", 16), ("a_ln_b", 16), ("a_b2", 16)):
                off[(l, name)] = n
                n += w
        else:
            off[(l, "b_conv")] = n
            n += 48
    return off, n


POFF, NPRM = _param_layout()


def _vec_cols(v):
    v = np.asarray(v, dtype=np.float32)
    return np.ascontiguousarray(v.reshape(-1, 128).T)


def _pack_params(inp):
    prm = np.zeros((128, NPRM), dtype=np.float32)

    def put(l, name, arr):
        a = _vec_cols(arr)
        prm[:, POFF[(l, name)]:POFF[(l, name)] + a.shape[1]] = a

    for l in range(DEPTH):
        put(l, "modb", inp["mod_b"][l])
        for nm in ("pre_mix_g", "post_mix_g", "pre_ffn_g", "post_ffn_g"):
            put(l, nm, inp[nm][l])
        j = l // 2
        if l % 2 == 0:
            put(l, "a_b1", inp["a_b1"][j])
            put(l, "a_dw", inp["a_dw"][j].reshape(-1))
            put(l, "a_dwb", inp["a_dwb"][j])
            put(l, "a_ln_g", inp["a_ln_g"][j])
            put(l, "a_ln_b", inp["a_ln_b"][j])
            put(l, "a_b2", inp["a_b2"][j])
        else:
            put(l, "b_conv", inp["b_conv"][j].reshape(-1))
    return prm


class Res:
    __slots__ = ("name", "w", "r")

    def __init__(self, name):
        self.name = name
        self.w = None
        self.r = {}


def _flat(x):
    out = []
    for e in x:
        if isinstance(e, (list, tuple)):
            out.extend(_flat(e))
        else:
            out.append(e)
    return out


class Trk:
    def __init__(self, nc, stack, dry=False):
        self.nc = nc
        self.dry = dry
        self.stack = stack
        self.eng = {"pe": nc.tensor, "act": nc.scalar, "dve": nc.vector, "pool": nc.gpsimd, "sp": nc.sync}
        self.sem = {}
        if not dry:
            for k in self.eng:
                self.sem[k] = stack.enter_context(nc.semaphore("s_" + k))
        self.cnt = {k: 0 for k in self.eng}
        self.waited = {k: {} for k in self.eng}
        self.nwaits = 0
        self.nops = 0
        self.self_sync = {"pe": False, "act": True, "dve": True, "pool": True, "sp": True}

    def dsem(self, name):
        if self.dry:
            return [None, 0]
        return [self.stack.enter_context(self.nc.semaphore(name)), 0]

    def _wait(self, e, sem, val):
        key = id(sem)
        if self.waited[e].get(key, 0) >= val:
            return
        self.eng[e].wait_ge(sem, val)
        self.waited[e][key] = val
        self.nwaits += 1

    def op(self, e, fn, reads=(), writes=(), dsem=None, ss=True):
        if self.dry:
            return None
        reads = _flat(reads)
        writes = _flat(writes)
        deps = []
        for r in reads:
            if r.w is not None:
                deps.append(r.w)
        for r in writes:
            if r.w is not None:
                deps.append(r.w)
            deps.extend(r.r.values())
        for (de, sem, val) in deps:
            if de == e and (not ss or not self.self_sync[e]):
                continue
            self._wait(e, sem, val)
        ins = fn()
        self.nops += 1
        if dsem is not None:
            for i_ in (ins if isinstance(ins, list) else [ins]):
                dsem[1] += 16
                i_.then_inc(dsem[0], 16)
            me = ("dma", dsem[0], dsem[1])
        else:
            self.cnt[e] += 1
            ins.then_inc(self.sem[e], 1)
            me = (e, self.sem[e], self.cnt[e])
        for r in reads:
            r.r[(me[0], id(me[1]))] = me
        for r in writes:
            r.w = me
            r.r = {}
        return me

    def wait_for(self, e, dep):
        if self.dry or dep is None:
            return
        self._wait(e, dep[1], dep[2])


def build_nc(layers=tuple(range(DEPTH)), first=True):
    nc = bass.Bass("TRN2", target_bir_lowering=False)
    xT = nc.dram_tensor("xT", [D, TC], F32, kind="ExternalInput").ap()
    cT = nc.dram_tensor("cT", [128, NCH], F32, kind="ExternalInput").ap()
    prm_d = nc.dram_tensor("prm", [128, NPRM], F32, kind="ExternalInput").ap()
    mod_w = nc.dram_tensor("mod_w", [DEPTH, D, 6 * D], F32, kind="ExternalInput").ap()
    a_w1 = nc.dram_tensor("a_w1", [2, D, 2 * D], F32, kind="ExternalInput").ap()
    a_w2 = nc.dram_tensor("a_w2", [2, D, D], F32, kind="ExternalInput").ap()
    b_w_in = nc.dram_tensor("b_w_in", [2, D, 3 * D], F32, kind="ExternalInput").ap()
    b_w_out = nc.dram_tensor("b_w_out", [2, D, D], F32, kind="ExternalInput").ap()
    f_w1 = nc.dram_tensor("f_w1", [DEPTH, D, DFF], F32, kind="ExternalInput").ap()
    f_w2 = nc.dram_tensor("f_w2", [DEPTH, DFF, D], F32, kind="ExternalInput").ap()
    ident_d = nc.dram_tensor("ident", [128, 128], F32, kind="ExternalInput").ap()
    outT = nc.dram_tensor("outT", [D, TC], F32, kind="ExternalOutput").ap()

    with ExitStack() as st_:
        sb = lambda name, shape, dt: st_.enter_context(nc.sbuf_tensor(name, shape, dt))
        ring = sb("ring", [128, RING, 16, 2, 128], BF16)
        hb = sb("hb", [128, NCH, TB], BF16)
        ub = sb("ub", [128, KFB, TB], BF16)
        BIG = sb("BIG", [128, NCH, TB], F32)
        tf = sb("tf", [128, 4, TFW], F32)
        sqb = sb("sqb", [128, 4, TB], BF16)
        stt = sb("stt", [128, 6, TW], F32)
        v16 = sb("v16", [128, 3, TFW], BF16)
        dg = sb("dg", [128, 2, NPE, 128], BF16)
        ident = sb("ident_s", [128, 128], F32)
        hv = sb("hv", [128, NCH, 32], F32)
        hp = sb("hp", [128, NCH, 2], F32)
        prm = sb("prm_s", [128, NPRM], F32)
        modT = sb("modT", [128, 2, 96], F32)
        der = sb("der", [128, 2, 64], F32)
        cab = sb("cab", [128, NCH], BF16)
        cTf = sb("cTf", [128, NCH], F32)
        ones = sb("ones", [128, 128], BF16)
        epsr = sb("epsr", [128, 2], F32)
        ps = st_.enter_context(nc.psum_tensor("ps", [128, 8, 512], F32))

        def emit(T, W):
            R_ring = [Res("ring%d" % i) for i in range(RING)]
            S_ring = [T.dsem("dring%d" % i) for i in range(RING)]
            R_hb = [[Res("hb") for t in range(NT)] for c in range(NCH)]
            R_ub = [[Res("ub") for t in range(NT)] for k in range(KFB)]
            R_big = [[Res("big") for t in range(NT)] for c in range(NCH)]
            S_big = [T.dsem("dbig%d" % c) for c in range(NCH)]
            R_tf = [[Res("tf%d" % i)] + ([Res("tft") for t in range(NT)] if i >= 2 else []) for i in range(4)]
            S_tf = [T.dsem("dtf%d" % i) for i in range(4)]
            R_sqb = [Res("sqb%d" % i) for i in range(4)]
            R_st = [Res("st%d" % i) for i in range(6)]
            R_v16 = [Res("v16_0"), Res("v16_1"), Res("v16_2")]
            R_dg = [Res("dg0"), Res("dg1")]
            R_ident = Res("ident")
            S_ident = T.dsem("dident")
            R_bank = [Res("bank%d" % i) for i in range(8)]
            R_hv = [Res("hv") for c in range(NCH)]
            R_hp = [Res("hp") for c in range(NCH)]
            R_prm = Res("prm")
            S_prm = T.dsem("dprm")
            R_cT = Res("cT")
            S_cT = T.dsem("dcT")
            R_cab = Res("cab")
            R_ones = Res("ones")
            R_modT = [Res("modT0"), Res("modT1")]
            R_der = [Res("der0"), Res("der1")]
            R_x = [[Res("x") for b in range(NBLK)] for c in range(NCH)]
            S_x = [[T.dsem("dx%d_%d" % (c, b)) for b in range(NBLK)] for c in range(NCH)]
            state = {"bank": 0, "sq": 0, "st": 0}

            def nbank():
                b = MAINB[state["bank"] % len(MAINB)]
                state["bank"] += 1
                return b

            def nsq():
                i = state["sq"] % 4
                state["sq"] += 1
                return i

            def nst():
                i = state["st"] % 6
                state["st"] += 1
                return i

            def P(l, name, lo, n=1):
                o = POFF[(l, name)] + lo
                return prm[:, o:o + n]

            tok = lambda t: slice(TOFF[t], TOFF[t + 1])
            rows = lambda c: slice(c * 128, (c + 1) * 128)
            cols = lambda b: slice(b * TB, (b + 1) * TB)

            def lhs16(s, c, jj):
                return ring[:, s, c, jj, :]

            def lhs8(s, k, jj):
                return ring[:, s, 2 * k + jj // 2, jj % 2, :]

            def issue_unit(desc, s):
                kind = desc[0]
                dst = ring[:, s]
                if kind == "mod":
                    _, l, u = desc
                    src = mod_w[l, :, u * 256:(u + 1) * 256].rearrange("(c p) (h n) -> p c h n", p=128, h=2)
                elif kind == "a1":
                    _, la, j = desc
                    src = [a_w1[la, :, h * D + j * 128:h * D + (j + 1) * 128].rearrange("(c p) n -> p c n", p=128) for h in range(2)]
                elif kind == "a2":
                    _, la, eu = desc
                    src = a_w2[la, :, eu * 256:(eu + 1) * 256].rearrange("(c p) (h n) -> p c h n", p=128, h=2)
                elif kind == "bcx":
                    _, lb, j = desc
                    src = [b_w_in[lb, :, (h + 1) * D + j * 128:(h + 1) * D + (j + 1) * 128].rearrange("(c p) n -> p c n", p=128) for h in range(2)]
                elif kind == "bb":
                    _, lb, eu = desc
                    src = b_w_in[lb, :, eu * 256:(eu + 1) * 256].rearrange("(c p) (h n) -> p c h n", p=128, h=2)
                elif kind == "bo":
                    _, lb, eu = desc
                    src = b_w_out[lb, :, eu * 256:(eu + 1) * 256].rearrange("(c p) (h n) -> p c h n", p=128, h=2)
                elif kind == "f1":
                    _, l, fb, u = desc
                    c0 = fb * FB + u * 256
                    src = f_w1[l, :, c0:c0 + 256].rearrange("(c p) (h n) -> p c h n", p=128, h=2)
                elif kind == "f2":
                    _, l, fb, u = desc
                    src = f_w2[l, fb * FB:(fb + 1) * FB, u * 512:(u + 1) * 512].rearrange(
                        "(k p) n -> p k n", p=128)
                else:
                    raise ValueError(kind)
                if isinstance(src, list):
                    T.op("pool", lambda: [nc.gpsimd.dma_start(out=ring[:, s, :, h, :], in_=src[h]) for h in range(2)],
                         writes=[R_ring[s]], dsem=S_ring[s])
                else:
                    T.op("pool", lambda: nc.gpsimd.dma_start(out=dst, in_=src), writes=[R_ring[s]], dsem=S_ring[s])

            W.issue = issue_unit

            def group(out_ap, lhs_fn, rhs_fn, n):
                ins = None
                for c in range(n):
                    ins = nc.tensor.matmul(out_ap, lhsT=lhs_fn(c), rhs=rhs_fn(c), start=(c == 0), stop=(c == n - 1))
                return ins

            def setup():
                T.op("sp", lambda: nc.sync.dma_start(out=prm[:], in_=prm_d), writes=[R_prm], dsem=S_prm)
                T.op("sp", lambda: nc.sync.dma_start(out=cTf[:], in_=cT), writes=[R_cT], dsem=S_cT)
                T.op("sp", lambda: nc.sync.dma_start(out=ident[:], in_=ident_d), writes=[R_ident], dsem=S_ident)
                T.op("dve", lambda: nc.vector.memset(ones[:], 1.0), writes=[R_ones])
                T.op("dve", lambda: nc.vector.memset(epsr[:, 0:1], RMS_EPS), writes=[R_ones])
                T.op("dve", lambda: nc.vector.memset(epsr[:, 1:2], LN_EPS), writes=[R_ones])
                T.op("act", lambda: nc.scalar.activation(out=cab[:], in_=cTf[:], func=AF.Silu), reads=[R_cT], writes=[R_cab])

            def mod_unit(l, u):
                s = W.get(("mod", l, u))

                def f():
                    ins = None
                    for jj in range(2):
                        col = u * 2 + jj
                        ins = group(ps[:, MODB, col:col + 1], lambda c: lhs16(s, c, jj), lambda c: cab[:, c:c + 1], NCH)
                    return ins
                T.op("pe", f, reads=[R_ring[s], R_cab], writes=[R_bank[MODB]])

            MOD_PIECES = [(0, 32, 16, (0, 16, "pre_mix_g", True)), (32, 48, 24, (1, 32, "post_mix_g", False)),
                          (48, 80, 40, (2, 64, "pre_ffn_g", True)), (80, 96, 48, (3, 80, "post_ffn_g", False))]

            def mod_piece(l, pi):
                p = l % 2
                c0, c1, _, (i, sc0, gname, is_gs) = MOD_PIECES[pi]
                T.op("dve", lambda: nc.vector.tensor_tensor(out=modT[:, p, c0:c1], in0=ps[:, MODB, c0:c1], in1=P(l, "modb", c0, c1 - c0), op=ALU.add),
                     reads=[R_bank[MODB], R_prm], writes=[R_modT[p]])
                o = der[:, p, i * 16:(i + 1) * 16]
                if is_gs:
                    fn = lambda: nc.vector.scalar_tensor_tensor(out=o, in0=modT[:, p, sc0:sc0 + 16], scalar=1.0, in1=P(l, gname, 0, 16),
                                                                op0=ALU.add, op1=ALU.mult)
                else:
                    fn = lambda: nc.vector.tensor_tensor(out=o, in0=modT[:, p, sc0:sc0 + 16], in1=P(l, gname, 0, 16), op=ALU.mult)
                T.op("dve", fn, reads=[R_modT[p], R_prm], writes=[R_der[p]])

            def mod_finish(l):
                for pi in range(4):
                    mod_piece(l, pi)

            lazy = {"l": None, "next_u": 48, "next_piece": 4}

            def mod_tick():
                if lazy["l"] is None or lazy["next_u"] >= 48:
                    return
                mod_unit(lazy["l"], lazy["next_u"])
                lazy["next_u"] += 1
                while lazy["next_piece"] < 4 and MOD_PIECES[lazy["next_piece"]][2] <= lazy["next_u"]:
                    mod_piece(lazy["l"], lazy["next_piece"])
                    lazy["next_piece"] += 1

            def mod_need(pi):
                while lazy["l"] is not None and lazy["next_piece"] <= pi:
                    mod_tick()

            S_fb = [T.dsem("dfb%d" % i) for i in range(14)]

            def sbuf_f32(i):
                if i < 8:
                    ap = hb[:, 2 * i:2 * i + 2, :].rearrange("p a b -> p (a b)").bitcast(F32)
                    res = [R_hb[2 * i], R_hb[2 * i + 1]]
                elif i < 12:
                    j = i - 8
                    ap = ub[:, 2 * j:2 * j + 2, :].rearrange("p a b -> p (a b)").bitcast(F32)
                    res = R_ub[2 * j] + R_ub[2 * j + 1]
                else:
                    ap = tf[:, i - 12, 0:TB]
                    res = [R_tf[i - 12]]
                return ap, res, S_fb[i]

            def stat_chunk(c):
                i = nsq()
                T.op("act", lambda: nc.scalar.activation(out=sqb[:, i, :], in_=BIG[:, c, :], func=AF.Square),
                     reads=R_big[c], writes=[R_sqb[i]])

                def f():
                    ins = None
                    for t in range(NT):
                        ins = nc.tensor.matmul(ps[:, STATB[t], 0:TWS[t]], lhsT=ones[:], rhs=sqb[:, i, tok(t)],
                                               start=(c == 0), stop=(c == NCH - 1))
                    return ins
                T.op("pe", f, reads=[R_sqb[i], R_ones], writes=[R_bank[b] for b in STATB])

            STAT_LAG = 2
            pend = []

            def stat_lag(c):
                pend.append(c)
                if len(pend) > STAT_LAG:
                    stat_chunk(pend.pop(0))

            def stat_flush():
                while pend:
                    stat_chunk(pend.pop(0))

            def rstd_finish():
                stat_flush()
                for t in range(NT):
                    T.op("act", lambda: nc.scalar.activation(out=tf[:, 3, tok(t)], in_=ps[:, STATB[t], 0:TWS[t]], func=AF.Sqrt,
                                                             scale=1.0 / D, bias=epsr[:, 0:1]),
                         reads=[R_bank[STATB[t]], R_ones], writes=[R_tf[3][1 + t]])
                for t in range(NT):
                    T.op("dve", lambda: nc.vector.reciprocal(out=tf[:, 2, tok(t)], in_=tf[:, 3, tok(t)]),
                         reads=[R_tf[3][1 + t]], writes=[R_tf[2][1 + t]])

            def prenorm(src, blk, gs, sh, rd_gs, rd_sh, mode):
                if mode == "load":
                    for c in range(NCH):
                        T.op("sp", lambda: nc.sync.dma_start(out=BIG[:, c, :], in_=src[rows(c), cols(blk)]),
                             reads=[R_x[c][blk]], writes=R_big[c], dsem=S_big[c])
                if mode != "stats":
                    for c in range(NCH):
                        stat_chunk(c)
                rstd_finish()
                for t in range(NT):
                    for c in range(NCH):
                        r = c % 2
                        T.op("dve", lambda: nc.vector.tensor_tensor(out=tf[:, r, tok(t)], in0=BIG[:, c, tok(t)], in1=tf[:, 2, tok(t)], op=ALU.mult),
                             reads=[R_big[c][t], R_tf[2][1 + t]], writes=[R_tf[r]])
                        T.op("act", lambda: nc.scalar.activation(out=hb[:, c, tok(t)], in_=tf[:, r, tok(t)], func=AF.Identity,
                                                                 scale=gs[:, c:c + 1], bias=sh[:, c:c + 1]),
                             reads=[R_tf[r], rd_gs, rd_sh], writes=[R_hb[c][t]])

            def postnorm_residual(src, blk, gp, rd_gp, keep, next_load=None):
                mod_need(1 if keep else 3)
                rstd_finish()
                if keep:
                    kb = [sbuf_f32(8 + i) for i in range(4)] + [sbuf_f32(12), sbuf_f32(13)]
                    nb = len(kb)

                    def xload(c):
                        ap, res, ds = kb[c % nb]
                        T.op("sp", lambda: nc.sync.dma_start(out=ap, in_=src[rows(c), cols(blk)]),
                             reads=[R_x[c][blk]], writes=res, dsem=ds)
                    for c in range(nb):
                        xload(c)
                    for c in range(NCH):
                        ap, res, ds = kb[c % nb]
                        T.op("dve", lambda: nc.vector.scalar_tensor_tensor(out=BIG[:, c, :], in0=BIG[:, c, :], scalar=gp[:, c:c + 1],
                                                                          in1=tf[:, 2, 0:TB], op0=ALU.mult, op1=ALU.mult),
                             reads=R_big[c] + [R_tf[2], rd_gp], writes=R_big[c])
                        en, E = ("dve", nc.vector)
                        T.op(en, lambda: E.tensor_tensor(out=BIG[:, c, :], in0=BIG[:, c, :], in1=ap, op=ALU.add),
                             reads=R_big[c] + res, writes=R_big[c])
                        T.op("sp", lambda: nc.sync.dma_start(out=outT[rows(c), cols(blk)], in_=BIG[:, c, :]),
                             reads=R_big[c], writes=[R_x[c][blk]], dsem=S_x[c][blk])
                        if c + nb < NCH:
                            xload(c + nb)
                        stat_chunk(c)
                else:
                    for c in range(NCH):
                        ap, res, _ = sbuf_f32(c % 14)
                        T.op("dve", lambda: nc.vector.scalar_tensor_tensor(out=ap, in0=BIG[:, c, :], scalar=gp[:, c:c + 1],
                                                                          in1=tf[:, 2, 0:TB], op0=ALU.mult, op1=ALU.mult),
                             reads=R_big[c] + [R_tf[2], rd_gp], writes=res)
                        T.op("pool", lambda: nc.gpsimd.dma_start(out=outT[rows(c), cols(blk)], in_=ap, accum_op=ALU.add),
                             reads=res, writes=[R_x[c][blk]], dsem=S_x[c][blk])
                        if next_load is not None:
                            next_load(c)

            def make_next_load(nsrc, nblk):
                def nl(c):
                    T.op("sp", lambda: nc.sync.dma_start(out=BIG[:, c, :], in_=nsrc[rows(c), cols(nblk)]),
                         reads=[R_x[c][nblk]], writes=R_big[c], dsem=S_big[c])
                return nl

            def mixer_a(l, blk, src, preloaded):
                la = l // 2
                p = l % 2
                prenorm(src, blk, der[:, p, 0:16], modT[:, p, 0:16], R_der[p], R_modT[p], "loaded" if preloaded else "load")
                def a_glu(j):
                    vq = j % 3
                    dq = j % 2
                    for k in range(NPE):
                        T.op("act", lambda: nc.scalar.activation(out=dg[:, dq, k, :], in_=ident[:], func=AF.Identity,
                                                                 scale=P(l, "a_dw", k * 16 + j)),
                             reads=[R_ident, R_prm], writes=[R_dg[dq]], ss=(k == 0))
                    mod_tick()
                    s = W.get(("a1", la, j))
                    if blk == 0:
                        T.op("dve", lambda: nc.vector.memset(v16[:, vq, 0:32], 0.0), writes=[R_v16[vq]])
                    else:
                        T.op("act", lambda: nc.scalar.copy(out=v16[:, vq, 2:32], in_=hv[:, j, 0:30]),
                             reads=[R_hv[j]], writes=[R_v16[vq]])
                    for t in range(NT):
                        ba = nbank()
                        T.op("pe", lambda: group(ps[:, ba, 0:TWS[t]], lambda c: ring[:, s, c, 0, :], lambda c: hb[:, c, tok(t)], NCH),
                             reads=[R_ring[s]] + [R_hb[c_][t] for c_ in range(NCH)], writes=[R_bank[ba]])
                        bg = nbank()
                        T.op("pe", lambda: group(ps[:, bg, 0:TWS[t]], lambda c: ring[:, s, c, 1, :], lambda c: hb[:, c, tok(t)], NCH),
                             reads=[R_ring[s]] + [R_hb[c_][t] for c_ in range(NCH)], writes=[R_bank[bg]])
                        ka = nst()
                        T.op("act", lambda: nc.scalar.activation(out=stt[:, ka, 0:TWS[t]], in_=ps[:, ba, 0:TWS[t]], func=AF.Identity,
                                                                 bias=P(l, "a_b1", j), scale=1.0),
                             reads=[R_bank[ba], R_prm], writes=[R_st[ka]])
                        ks = nst()
                        T.op("act", lambda: nc.scalar.activation(out=stt[:, ks, 0:TWS[t]], in_=ps[:, bg, 0:TWS[t]], func=AF.Sigmoid,
                                                                 bias=P(l, "a_b1", 16 + j), scale=1.0),
                             reads=[R_bank[bg], R_prm], writes=[R_st[ks]])
                        T.op("dve", lambda: nc.vector.tensor_tensor(out=v16[:, vq, 32 + TOFF[t]:32 + TOFF[t + 1]], in0=stt[:, ka, 0:TWS[t]],
                                                                   in1=stt[:, ks, 0:TWS[t]], op=ALU.mult),
                             reads=[R_st[ka], R_st[ks]], writes=[R_v16[vq]])
                    if blk < NBLK - 1:
                        T.op("act", lambda: nc.scalar.copy(out=hv[:, j, 0:30], in_=v16[:, vq, 32 + TB - 30:32 + TB]),
                             reads=[R_v16[vq]], writes=[R_hv[j]])

                def a_conv(j):
                    vq = j % 3
                    dq = j % 2
                    for t in range(NT):
                        b = nbank()
                        T.op("pe", lambda: group(ps[:, b, 0:TWS[t]], lambda k: dg[:, dq, k, :],
                                                 lambda k: v16[:, vq, 2 + k + TOFF[t]:2 + k + TOFF[t + 1]], NPE),
                             reads=[R_dg[dq], R_v16[vq]], writes=[R_bank[b]])
                        T.op("act", lambda: nc.scalar.activation(out=BIG[:, j, tok(t)], in_=ps[:, b, 0:TWS[t]], func=AF.Identity,
                                                                 bias=P(l, "a_dwb", j), scale=1.0),
                             reads=[R_bank[b], R_prm], writes=[R_big[j][t]])
                    for k in range(NPE, CONF_K):
                        T.op("dve", lambda: nc.vector.scalar_tensor_tensor(out=BIG[:, j, :], in0=v16[:, vq, 2 + k:2 + k + TB],
                                                                          scalar=P(l, "a_dw", k * 16 + j), in1=BIG[:, j, :],
                                                                          op0=ALU.mult, op1=ALU.add),
                             reads=[R_v16[vq], R_prm] + R_big[j], writes=R_big[j], ss=(k == NPE))

                for j in range(NCH + 1):
                    if j < NCH:
                        a_glu(j)
                    if j >= 1:
                        a_conv(j - 1)
                QB = [1, 2, 3]
                for j in range(NCH):
                    i0 = nsq()
                    i1 = nsq()
                    T.op("dve", lambda: nc.vector.tensor_copy(out=sqb[:, i0, :], in_=BIG[:, j, :]), reads=R_big[j], writes=[R_sqb[i0]])
                    T.op("act", lambda: nc.scalar.activation(out=sqb[:, i1, :], in_=BIG[:, j, :], func=AF.Square),
                         reads=R_big[j], writes=[R_sqb[i1]])

                    def f():
                        ins = None
                        for t in range(NT):
                            nc.tensor.matmul(ps[:, STATB[t], 0:TWS[t]], lhsT=ones[:], rhs=sqb[:, i0, tok(t)], start=(j == 0), stop=(j == NCH - 1))
                            ins = nc.tensor.matmul(ps[:, QB[t], 0:TWS[t]], lhsT=ones[:], rhs=sqb[:, i1, tok(t)], start=(j == 0), stop=(j == NCH - 1))
                        return ins
                    T.op("pe", f, reads=[R_sqb[i0], R_sqb[i1], R_ones], writes=[R_bank[b] for b in STATB + QB])
                for t in range(NT):
                    tk = tok(t)
                    T.op("dve", lambda: nc.vector.tensor_scalar(out=tf[:, 3, tk], in0=ps[:, STATB[t], 0:TWS[t]], scalar1=1.0 / D, scalar2=None, op0=ALU.mult),
                         reads=[R_bank[STATB[t]]], writes=[R_tf[3][1 + t]])
                    T.op("dve", lambda: nc.vector.tensor_tensor(out=tf[:, 0, tk], in0=tf[:, 3, tk], in1=tf[:, 3, tk], op=ALU.mult),
                         reads=[R_tf[3][1 + t]], writes=[R_tf[0]])
                    T.op("dve", lambda: nc.vector.scalar_tensor_tensor(out=tf[:, 0, tk], in0=ps[:, QB[t], 0:TWS[t]], scalar=1.0 / D,
                                                                      in1=tf[:, 0, tk], op0=ALU.mult, op1=ALU.subtract),
                         reads=[R_bank[QB[t]], R_tf[0]], writes=[R_tf[0]])
                    T.op("act", lambda: nc.scalar.activation(out=tf[:, 0, tk], in_=tf[:, 0, tk], func=AF.Sqrt, scale=1.0, bias=epsr[:, 1:2]),
                         reads=[R_tf[0], R_ones], writes=[R_tf[0]])
                    T.op("dve", lambda: nc.vector.reciprocal(out=tf[:, 2, tk], in_=tf[:, 0, tk]), reads=[R_tf[0]], writes=[R_tf[2][1 + t]])
                    T.op("dve", lambda: nc.vector.scalar_tensor_tensor(out=tf[:, 3, tk], in0=tf[:, 3, tk], scalar=-1.0, in1=tf[:, 2, tk],
                                                                      op0=ALU.mult, op1=ALU.mult),
                         reads=[R_tf[3][1 + t], R_tf[2][1 + t]], writes=[R_tf[3][1 + t]])
                for t in range(NT):
                    tk = tok(t)
                    for j in range(NCH):
                        i = nst()
                        T.op("dve", lambda: nc.vector.tensor_tensor(out=stt[:, i, 0:TWS[t]], in0=BIG[:, j, tk], in1=tf[:, 2, tk], op=ALU.mult),
                             reads=[R_big[j][t], R_tf[2][1 + t]], writes=[R_st[i]])
                        T.op("dve", lambda: nc.vector.tensor_tensor(out=stt[:, i, 0:TWS[t]], in0=stt[:, i, 0:TWS[t]], in1=tf[:, 3, tk], op=ALU.add),
                             reads=[R_st[i], R_tf[3][1 + t]], writes=[R_st[i]])
                        T.op("act", lambda: nc.scalar.activation(out=hb[:, j, tk], in_=stt[:, i, 0:TWS[t]], func=AF.Silu,
                                                                 scale=P(l, "a_ln_g", j), bias=P(l, "a_ln_b", j)),
                             reads=[R_st[i], R_prm], writes=[R_hb[j][t]])
                for eu in range(8):
                    mod_tick()
                    s = W.get(("a2", la, eu))
                    for jj in range(2):
                        e = eu * 2 + jj
                        for t in range(NT):
                            b = nbank()
                            T.op("pe", lambda: group(ps[:, b, 0:TWS[t]], lambda c: lhs16(s, c, jj), lambda c: hb[:, c, tok(t)], NCH),
                                 reads=[R_ring[s]] + [R_hb[c_][t] for c_ in range(NCH)], writes=[R_bank[b]])
                            T.op("act", lambda: nc.scalar.activation(out=BIG[:, e, tok(t)], in_=ps[:, b, 0:TWS[t]], func=AF.Identity,
                                                                     bias=P(l, "a_b2", e), scale=1.0),
                                 reads=[R_bank[b], R_prm], writes=[R_big[e][t]])
                        stat_lag(e)
                postnorm_residual(src, blk, der[:, p, 16:32], R_der[p], keep=True)

            def mixer_b(l, blk, src, preloaded):
                lb = l // 2
                p = l % 2
                prenorm(src, blk, der[:, p, 0:16], modT[:, p, 0:16], R_der[p], R_modT[p], "loaded" if preloaded else "load")
                for j in range(NCH):
                    mod_tick()
                    s = W.get(("bcx", lb, j))
                    vr = j % 2
                    if blk == 0:
                        T.op("dve", lambda: nc.vector.memset(tf[:, vr, 0:32], 0.0), writes=[R_tf[vr]])
                    else:
                        T.op("dve", lambda: nc.vector.tensor_copy(out=tf[:, vr, 30:32], in_=hp[:, j, 0:2]),
                             reads=[R_hp[j]], writes=[R_tf[vr]])
                    for t in range(NT):
                        bc = nbank()
                        T.op("pe", lambda: group(ps[:, bc, 0:TWS[t]], lambda c: ring[:, s, c, 0, :], lambda c: hb[:, c, tok(t)], NCH),
                             reads=[R_ring[s]] + [R_hb[c_][t] for c_ in range(NCH)], writes=[R_bank[bc]])
                        bx = nbank()
                        T.op("pe", lambda: group(ps[:, bx, 0:TWS[t]], lambda c: ring[:, s, c, 1, :], lambda c: hb[:, c, tok(t)], NCH),
                             reads=[R_ring[s]] + [R_hb[c_][t] for c_ in range(NCH)], writes=[R_bank[bx]])
                        k = nst()
                        T.op("act", lambda: nc.scalar.copy(out=stt[:, k, 0:TWS[t]], in_=ps[:, bc, 0:TWS[t]]), reads=[R_bank[bc]], writes=[R_st[k]])
                        T.op("dve", lambda: nc.vector.tensor_tensor(out=tf[:, vr, 32 + TOFF[t]:32 + TOFF[t + 1]], in0=ps[:, bx, 0:TWS[t]],
                                                                   in1=stt[:, k, 0:TWS[t]], op=ALU.mult),
                             reads=[R_bank[bx], R_st[k]], writes=[R_tf[vr]])
                    if blk < NBLK - 1:
                        T.op("act", lambda: nc.scalar.copy(out=hp[:, j, 0:2], in_=tf[:, vr, 32 + TB - 2:32 + TB]),
                             reads=[R_tf[vr]], writes=[R_hp[j]])
                    T.op("act", lambda: nc.scalar.activation(out=BIG[:, j, :], in_=tf[:, vr, 30:30 + TB], func=AF.Identity,
                                                             scale=P(l, "b_conv", 0 * 16 + j)),
                         reads=[R_tf[vr], R_prm], writes=R_big[j])
                    for k in range(1, 3):
                        T.op("dve", lambda: nc.vector.scalar_tensor_tensor(out=BIG[:, j, :], in0=tf[:, vr, 30 + k:30 + k + TB],
                                                                          scalar=P(l, "b_conv", k * 16 + j), in1=BIG[:, j, :],
                                                                          op0=ALU.mult, op1=ALU.add),
                             reads=[R_tf[vr], R_prm] + R_big[j], writes=R_big[j])
                for eu in range(8):
                    mod_tick()
                    s = W.get(("bb", lb, eu))
                    for jj in range(2):
                        e = eu * 2 + jj
                        for t in range(NT):
                            b = nbank()
                            T.op("pe", lambda: group(ps[:, b, 0:TWS[t]], lambda c: lhs16(s, c, jj), lambda c: hb[:, c, tok(t)], NCH),
                                 reads=[R_ring[s]] + [R_hb[c_][t] for c_ in range(NCH)], writes=[R_bank[b]])
                            T.op("dve", lambda: nc.vector.tensor_tensor(out=BIG[:, e, tok(t)], in0=ps[:, b, 0:TWS[t]], in1=BIG[:, e, tok(t)], op=ALU.mult),
                                 reads=[R_bank[b], R_big[e][t]], writes=[R_big[e][t]])
                for c in range(NCH):
                    T.op("act", lambda: nc.scalar.copy(out=hb[:, c, :], in_=BIG[:, c, :]), reads=R_big[c], writes=[R_hb[c]])
                for eu in range(8):
                    mod_tick()
                    s = W.get(("bo", lb, eu))
                    for jj in range(2):
                        e = eu * 2 + jj
                        for t in range(NT):
                            b = nbank()
                            T.op("pe", lambda: group(ps[:, b, 0:TWS[t]], lambda c: lhs16(s, c, jj), lambda c: hb[:, c, tok(t)], NCH),
                                 reads=[R_ring[s]] + [R_hb[c_][t] for c_ in range(NCH)], writes=[R_bank[b]])
                            T.op("act", lambda: nc.scalar.copy(out=BIG[:, e, tok(t)], in_=ps[:, b, 0:TWS[t]]),
                                 reads=[R_bank[b]], writes=[R_big[e][t]])
                        stat_lag(e)
                postnorm_residual(src, blk, der[:, p, 16:32], R_der[p], keep=True)

            def ffn(l, blk, next_mod, next_load):
                p = l % 2
                mod_need(2)
                prenorm(outT, blk, der[:, p, 32:48], modT[:, p, 48:64], R_der[p], R_modT[p], "stats")
                for fb in range(NFB):
                    for u in range(4):
                        mod_tick()
                        s = W.get(("f1", l, fb, u))
                        for jj in range(2):
                            kk = u * 2 + jj
                            for t in range(NT):
                                b = nbank()
                                T.op("pe", lambda: group(ps[:, b, 0:TWS[t]], lambda c: lhs16(s, c, jj), lambda c: hb[:, c, tok(t)], NCH),
                                     reads=[R_ring[s]] + [R_hb[c_][t] for c_ in range(NCH)], writes=[R_bank[b]])
                                i = nst()
                                T.op("act", lambda: nc.scalar.activation(out=stt[:, i, 0:TWS[t]], in_=ps[:, b, 0:TWS[t]], func=AF.Relu),
                                     reads=[R_bank[b]], writes=[R_st[i]])
                                T.op("dve", lambda: nc.vector.tensor_tensor(out=ub[:, kk, tok(t)], in0=stt[:, i, 0:TWS[t]], in1=stt[:, i, 0:TWS[t]], op=ALU.mult),
                                     reads=[R_st[i]], writes=[R_ub[kk][t]])
                        if next_mod is not None and u < 3:
                            mod_unit(next_mod, (blk * NFB + fb) * 3 + u)
                    for u in range(4):
                        mod_tick()
                        s = W.get(("f2", l, fb, u))
                        for jj in range(4):
                            e = u * 4 + jj
                            for t in range(NT):
                                b = nbank()
                                T.op("pe", lambda: group(ps[:, b, 0:TWS[t]], lambda k: lhs8(s, k, jj), lambda k: ub[:, k, tok(t)], KFB),
                                     reads=[R_ring[s]] + [R_ub[k][t] for k in range(KFB)], writes=[R_bank[b]])
                                if fb == 0:
                                    T.op("act", lambda: nc.scalar.copy(out=BIG[:, e, tok(t)], in_=ps[:, b, 0:TWS[t]]),
                                         reads=[R_bank[b]], writes=[R_big[e][t]])
                                else:
                                    T.op("dve", lambda: nc.vector.tensor_tensor(out=BIG[:, e, tok(t)], in0=ps[:, b, 0:TWS[t]], in1=BIG[:, e, tok(t)], op=ALU.add),
                                         reads=[R_bank[b], R_big[e][t]], writes=[R_big[e][t]])
                            if fb == NFB - 1:
                                stat_lag(e)
                postnorm_residual(outT, blk, der[:, p, 48:64], R_der[p], keep=False, next_load=next_load)

            setup()
            for li, l in enumerate(layers):
                if li == 0:
                    for u in range(16):
                        mod_unit(l, u)
                    mod_piece(l, 0)
                    lazy.update(l=l, next_u=16, next_piece=1)
                else:
                    mod_need(3)
                    mod_finish(l)
                nxt = layers[li + 1] if li + 1 < len(layers) else None
                for blk in range(NBLK):
                    src = xT if (first and li == 0) else outT
                    pre = state.get("preloaded", False)
                    if l % 2 == 0:
                        mixer_a(l, blk, src, pre)
                    else:
                        mixer_b(l, blk, src, pre)
                    if blk + 1 < NBLK:
                        nl = make_next_load(src, blk + 1)
                    elif li + 1 < len(layers):
                        nl = make_next_load(outT, 0)
                    else:
                        nl = None
                    state["preloaded"] = nl is not None
                    ffn(l, blk, nxt, nl)
            for c in range(NCH):
                for b in range(NBLK):
                    T.wait_for("sp", R_x[c][b].w)

        class WStream:
            def __init__(self, plan):
                self.plan = plan
                self.rec = []
                self.i = 0
                self.issued = 0
                self.issue = None

            def get(self, desc):
                if self.plan is None:
                    self.rec.append(desc)
                    return 0
                assert self.plan[self.i] == desc, (self.plan[self.i], desc)
                while self.issued < min(len(self.plan), self.i + RING):
                    self.issue(self.plan[self.issued], self.issued % RING)
                    self.issued += 1
                s = self.i % RING
                self.i += 1
                return s

        dryW = WStream(None)
        emit(Trk(nc, st_, dry=True), dryW)
        T = Trk(nc, st_)
        W = WStream(dryW.rec)
        emit(T, W)
        assert W.i == len(W.plan)
        build_nc.stats = dict(nops=T.nops, nwaits=T.nwaits, cnt=dict(T.cnt), units=len(W.plan))
    return nc


_WKEYS = ("mod_w", "a_w1", "a_w2", "b_w_in", "b_w_out", "f_w1", "f_w2")
LAYERS_PER_LAUNCH = DEPTH


def kernel(**inputs):
    inp = {k: np.asarray(v) for k, v in inputs.items()}
    x = inp["x"].astype(np.float32, copy=False)
    c = inp["c"].astype(np.float32, copy=False)
    prm = _pack_params(inp)
    wts = {k: np.ascontiguousarray(inp[k], dtype=np.float32) for k in _WKEYS}
    starts = [0, SEQ - TC]
    cur = []
    for k in range(N_CORES):
        b, h = k // 2, k % 2
        cur.append(np.ascontiguousarray(x[b, starts[h]:starts[h] + TC, :].T))
    for l0 in range(0, DEPTH, LAYERS_PER_LAUNCH):
        layers = tuple(range(l0, min(DEPTH, l0 + LAYERS_PER_LAUNCH)))
        nc = build_nc(layers=layers, first=True)
        in_maps = []
        for k in range(N_CORES):
            b = k // 2
            m = {"xT": cur[k], "cT": _vec_cols(c[b]), "prm": prm, "ident": np.eye(128, dtype=np.float32)}
            m.update(wts)
            in_maps.append(m)
        res = run_bass_kernel_spmd(nc, in_maps, core_ids=list(range(N_CORES)))
        cur = [np.asarray(res.results[k]["outT"]) for k in range(N_CORES)]
    out = np.empty((BATCH, SEQ, D), dtype=np.float32)
    for k in range(N_CORES):
        b, h = k // 2, k % 2
        if h == 0:
            out[b, 0:TC, :] = cur[k].T
        else:
            out[b, TC:SEQ, :] = cur[k][:, 2 * TC - SEQ:TC].T
    return out
```
